# Optimizing a Trainium2 kernel written in Bass

```python
import jax, jax.numpy as jnp
from jax import lax
import numpy as np

D_MODEL = 1024
BATCH = 8
SEQ = 2048
DEPTH = 2

PLE_DIM = 256
EPS = 1e-6
A_HEAD_DIM = 128
A_HEADS = (D_MODEL // 2) // A_HEAD_DIM
A_DIM = A_HEADS * A_HEAD_DIM
QKV_CONV_WIDTH = 4
CHUNK = 64
POOL_WINDOWS = (2, 4, 8, 16)
POOL_GROUPS = len(POOL_WINDOWS)
POOL_DIM = D_MODEL // 4
POOL_GROUP_DIM = POOL_DIM // POOL_GROUPS
CONV_HEADS = 4
CONV_DIM = D_MODEL // 4
CONV_WIDTH = 3
D_MIX = A_DIM + POOL_DIM + CONV_DIM
IN_SIZES = (A_DIM, A_DIM, A_DIM, A_DIM, A_HEADS, A_HEADS, POOL_DIM, CONV_DIM, CONV_DIM, CONV_DIM)
D_IN = sum(IN_SIZES)
D_FF = -(-8 * D_MODEL // (3 * 256)) * 256

kernel_name = 'hybrid_parallel_deltanet_pool_shortconv'


def rms_norm(x, g):
    xf = x.astype(jnp.float32)
    y = xf * lax.rsqrt(jnp.mean(xf * xf, axis=-1, keepdims=True) + EPS)
    return (y * g.astype(jnp.float32)).astype(x.dtype)


def causal_dwconv(x, w):
    K, C = w.shape
    return lax.conv_general_dilated(
        x, w[:, None, :].astype(x.dtype), window_strides=(1,), padding=[(K - 1, 0)],
        dimension_numbers=('NWC', 'WIO', 'NWC'), feature_group_count=C)


def l2norm(t):
    return t * lax.rsqrt(jnp.sum(t * t, axis=-1, keepdims=True) + EPS)


def chunk_gated_delta_rule(q, k, v, g, beta):
    Bn, S, H, DK = q.shape
    DV = v.shape[-1]
    N = S // CHUNK

    def to_chunks(t):
        t = t.reshape((Bn, N, CHUNK, H) + t.shape[3:])
        return jnp.moveaxis(t, 3, 1)

    q = to_chunks(q * (DK ** -0.5))
    k = to_chunks(k)
    v = to_chunks(v)
    g = to_chunks(g)
    beta = to_chunks(beta)
    gc = jnp.cumsum(g, axis=-1)
    kb = k * beta[..., None]
    vb = v * beta[..., None]
    causal_incl = jnp.tril(jnp.ones((CHUNK, CHUNK), dtype=bool))
    causal_strict = jnp.tril(jnp.ones((CHUNK, CHUNK), dtype=bool), -1)
    diff = gc[..., :, None] - gc[..., None, :]
    decay = jnp.exp(jnp.where(causal_incl, diff, -jnp.inf))
    lower = jnp.where(causal_strict, jnp.einsum('bhncd,bhnsd->bhncs', kb, k) * decay, 0.0)
    eye = jnp.eye(CHUNK, dtype=jnp.float32)
    tmat = lax.linalg.triangular_solve(eye + lower, jnp.broadcast_to(eye, lower.shape),
                                       left_side=True, lower=True, unit_diagonal=True)
    u = jnp.einsum('bhncs,bhnsv->bhncv', tmat, vb)
    w = jnp.einsum('bhncs,bhnsd->bhncd', tmat, kb * jnp.exp(gc)[..., None])
    attn = jnp.einsum('bhncd,bhnsd->bhncs', q, k) * decay

    def step(state, inp):
        q_i, k_i, u_i, w_i, gc_i, a_i = inp
        v_new = u_i - jnp.einsum('bhck,bhkv->bhcv', w_i, state)
        o_i = (jnp.einsum('bhck,bhkv->bhcv', q_i * jnp.exp(gc_i)[..., None], state)
               + jnp.einsum('bhcs,bhsv->bhcv', a_i, v_new))
        g_last = gc_i[..., -1]
        state = (state * jnp.exp(g_last)[..., None, None]
                 + jnp.einsum('bhck,bhcv->bhkv', k_i * jnp.exp(g_last[..., None] - gc_i)[..., None], v_new))
        return state, o_i

    xs = tuple(jnp.moveaxis(t, 2, 0) for t in (q, k, u, w, gc, attn))
    state0 = jnp.zeros((Bn, H, DK, DV), jnp.float32)
    _, o = lax.scan(step, state0, xs)
    return jnp.transpose(o, (1, 0, 3, 2, 4)).reshape(Bn, S, H, DV)


def gated_deltanet(q, k, v, z, a, b, conv_w, a_log, dt_bias, onorm_g):
    Bn, S, _ = q.shape
    qkv = jax.nn.silu(causal_dwconv(jnp.concatenate([q, k, v], axis=-1), conv_w))
    q, k, v = jnp.split(qkv.astype(jnp.float32), 3, axis=-1)
    hs = (Bn, S, A_HEADS, A_HEAD_DIM)
    q = l2norm(q.reshape(hs))
    k = l2norm(k.reshape(hs))
    v = v.reshape(hs)
    beta = jax.nn.sigmoid(b.astype(jnp.float32))
    g = -jnp.exp(a_log.astype(jnp.float32)) * jax.nn.softplus(a.astype(jnp.float32) + dt_bias.astype(jnp.float32))
    o = chunk_gated_delta_rule(q, k, v, g, beta)
    o = o * lax.rsqrt(jnp.mean(o * o, axis=-1, keepdims=True) + EPS) * onorm_g.astype(jnp.float32)
    o = o * jax.nn.silu(z.astype(jnp.float32).reshape(hs))
    return o.reshape(Bn, S, A_DIM).astype(z.dtype)


def multiscale_pool(h, pool_w, pool_scale):
    Bn, S, _ = h.shape
    hf = h.astype(jnp.float32)
    cs = jnp.cumsum(hf, axis=1)
    count = jnp.arange(1, S + 1, dtype=jnp.float32)[:, None]
    outs = []
    for gi, win in enumerate(POOL_WINDOWS):
        sl = slice(gi * POOL_GROUP_DIM, (gi + 1) * POOL_GROUP_DIM)
        csg = cs[..., sl]
        lag = jnp.pad(csg, ((0, 0), (win, 0), (0, 0)))[:, :S]
        mean = (csg - lag) / jnp.minimum(count, float(win))
        outs.append(mean - hf[..., sl])
    pooled = jnp.stack(outs, axis=2)
    y = jnp.einsum('bsgc,gcd->bsgd', pooled, pool_w.astype(jnp.float32)).reshape(Bn, S, POOL_DIM)
    return (y * pool_scale.astype(jnp.float32)).astype(h.dtype)


def short_gated_conv(gate_b, gate_c, hc, conv_w):
    return gate_b * causal_dwconv(gate_c * hc, conv_w)


def setup_inputs(seed: int = 0) -> dict:
    key = jax.random.key(seed)
    ks = jax.random.split(key, 24)
    f32 = jnp.float32
    nrm = lambda k, shape, scale: jax.random.normal(k, shape, f32) * scale
    dt = jnp.exp(jax.random.uniform(ks[5], (DEPTH, A_HEADS), f32, np.log(1e-3), np.log(1e-1)))
    return {
        'x': nrm(ks[0], (BATCH, SEQ, D_MODEL), 1.0),
        'p': nrm(ks[1], (DEPTH, BATCH, SEQ, PLE_DIM), 1.0),
        'norm1_g': 1.0 + nrm(ks[2], (DEPTH, D_MODEL), 0.02),
        'w_in': nrm(ks[3], (DEPTH, D_MODEL, D_IN), D_MODEL ** -0.5),
        'conv_qkv': nrm(ks[4], (DEPTH, QKV_CONV_WIDTH, 3 * A_DIM), QKV_CONV_WIDTH ** -0.5),
        'a_log': jnp.log(jax.random.uniform(ks[6], (DEPTH, A_HEADS), f32, 1.0, 16.0)),
        'dt_bias': jnp.log(jnp.expm1(dt)),
        'onorm_g': 1.0 + nrm(ks[7], (DEPTH, A_HEAD_DIM), 0.02),
        'pool_w': nrm(ks[8], (DEPTH, POOL_GROUPS, POOL_GROUP_DIM, POOL_GROUP_DIM), POOL_GROUP_DIM ** -0.5),
        'pool_scale': 1.0 + nrm(ks[9], (DEPTH, POOL_DIM), 0.02),
        'sconv_w': nrm(ks[10], (DEPTH, CONV_WIDTH, CONV_DIM), CONV_WIDTH ** -0.5),
        'w_out': nrm(ks[11], (DEPTH, D_MIX, D_MODEL), D_MIX ** -0.5),
        'norm2_g': 1.0 + nrm(ks[12], (DEPTH, D_MODEL), 0.02),
        'w_gate': nrm(ks[13], (DEPTH, D_MODEL, D_FF), D_MODEL ** -0.5),
        'w_up': nrm(ks[14], (DEPTH, D_MODEL, D_FF), D_MODEL ** -0.5),
        'w_down': nrm(ks[15], (DEPTH, D_FF, D_MODEL), D_FF ** -0.5),
        'ple_proj': nrm(ks[16], (DEPTH, PLE_DIM, D_MODEL), PLE_DIM ** -0.5),
        'ple_gate': nrm(ks[17], (DEPTH, D_MODEL, D_MODEL), D_MODEL ** -0.5),
        'final_g': 1.0 + nrm(ks[18], (D_MODEL,), 0.02),
    }


def reference(x, p, norm1_g, w_in, conv_qkv, a_log, dt_bias, onorm_g, pool_w, pool_scale,
              sconv_w, w_out, norm2_g, w_gate, w_up, w_down, ple_proj, ple_gate, final_g):
    offsets = [0]
    for s in IN_SIZES[:-1]:
        offsets.append(offsets[-1] + s)
    for i in range(DEPTH):
        h = rms_norm(x, norm1_g[i])
        proj = jnp.einsum('bsd,de->bse', h, w_in[i])
        q, k, v, z, a, b, hp, cb, cc, ch = jnp.split(proj, offsets[1:], axis=-1)
        o_a = gated_deltanet(q, k, v, z, a, b, conv_qkv[i], a_log[i], dt_bias[i], onorm_g[i])
        o_b = multiscale_pool(hp, pool_w[i], pool_scale[i])
        o_c = short_gated_conv(cb, cc, ch, sconv_w[i])
        mixed = jnp.concatenate([o_a, o_b, o_c], axis=-1)
        x = x + jnp.einsum('bse,ed->bsd', mixed, w_out[i])
        h = rms_norm(x, norm2_g[i])
        ff = jax.nn.silu(jnp.einsum('bsd,df->bsf', h, w_gate[i])) * jnp.einsum('bsd,df->bsf', h, w_up[i])
        x = x + jnp.einsum('bsf,fd->bsd', ff, w_down[i])
        gate = jax.nn.sigmoid(jnp.einsum('bsd,de->bse', x, ple_gate[i]).astype(jnp.float32)).astype(x.dtype)
        x = x + gate * jnp.einsum('bsq,qd->bsd', p[i], ple_proj[i])
    return rms_norm(x, final_g)
```

```python
import os
import numpy as np
import concourse.bass as bass
import concourse.mybir as mybir
from concourse.bass_utils import run_bass_kernel_spmd

F32 = mybir.dt.float32
BF16 = mybir.dt.bfloat16
AF = mybir.ActivationFunctionType
ALU = mybir.AluOpType

D = 1024
S = 2048
TT = 512
NT = S // TT
DIN = 3080
DFF = 2816
NF = DFF // 128
EPS = 1e-6
VL = 105
NV = 2 * VL + 8
N_CORES = 8
PRE_WIDTH = int(os.environ.get('K_PRE_WIDTH', '2'))
N_FILL = int(os.environ.get('K_FILL', '0'))
BURST_EVERY = int(os.environ.get('K_BURST_EVERY', '0'))
BURST_N = int(os.environ.get('K_BURST_N', '16'))
COLS_IN_A = bool(int(os.environ.get('K_COLS_IN_A', '1')))
SCAN_IN_B = bool(int(os.environ.get('K_SCAN_IN_B', '0')))
CHUNK_WIDTH = int(os.environ.get('K_CHUNK_WIDTH', '2'))
RAW_ENG = os.environ.get('K_RAW_ENG', 'dve')
SQ_ENG = os.environ.get('K_SQ_ENG', 'act')
B1_ENG = os.environ.get('K_B1_ENG', 'act')
BC_ENG = os.environ.get('K_BC_ENG', 'dve')
C_SPEED = int(os.environ.get('K_C_SPEED', '1'))
A_SPEED = int(os.environ.get('K_A_SPEED', '2'))
NA_BLOCKS = int(os.environ.get('K_NA_BLOCKS', '4'))
PAIR_CHUNKS = bool(int(os.environ.get('K_PAIR', '1')))
N_G4F = 6
N_G4B = 15
SAME_ENGINE_NOWAIT = bool(int(os.environ.get('K_SE_NOWAIT', '0')))


class Ref:
    __slots__ = ("ap", "keys")

    def __init__(self, ap, keys):
        self.ap = ap
        self.keys = keys if isinstance(keys, list) else [keys]


class Sched:
    def __init__(self, nc, n_dma_sems=8):
        self.nc = nc
        self.E = {"pe": nc.tensor, "act": nc.scalar, "dve": nc.vector, "pool": nc.gpsimd, "sp": nc.sync}
        self.sem = {}
        self.cnt = {}
        for e in self.E:
            self.sem[e] = nc.semaphore("s_" + e).__enter__()
            self.cnt[e] = 0
        self.seen = {e: {} for e in self.E}
        self.lastw = {}
        self.readers = {}
        self.dsem = {}
        self.dtot = {}
        self.drr = {}
        for q in ("sp", "pool"):
            self.dsem[q] = [nc.semaphore("d_%s%d" % (q, i)).__enter__() for i in range(n_dma_sems)]
            self.dtot[q] = [0] * n_dma_sems
            self.drr[q] = 0

    def _wait(self, e, dep):
        sem, val, src = dep
        if src == e and (e == "pe" or (SAME_ENGINE_NOWAIT and e in ("act", "dve"))):
            return
        k = id(sem)
        if self.seen[e].get(k, 0) >= val:
            return
        self.E[e].wait_ge(sem, val)
        self.seen[e][k] = val

    def _deps(self, reads, writes):
        deps = []
        for r in reads:
            for k in r.keys:
                if k in self.lastw:
                    deps.append(self.lastw[k])
        for w in writes:
            for k in w.keys:
                if k in self.lastw:
                    deps.append(self.lastw[k])
                rd = self.readers.get(k)
                if rd:
                    deps.extend(rd.values())
        return deps

    def _record(self, tag, reads, writes):
        for w in writes:
            for k in w.keys:
                self.lastw[k] = tag
                self.readers[k] = {}
        for r in reads:
            for k in r.keys:
                self.readers.setdefault(k, {})[id(tag[0])] = tag

    def op(self, e, fn, reads, writes, inc=True):
        for d in self._deps(reads, writes):
            self._wait(e, d)
        ins = fn()
        if inc:
            ins.then_inc(self.sem[e], 1)
            self.cnt[e] += 1
            tag = (self.sem[e], self.cnt[e], e)
        else:
            tag = (self.sem[e], self.cnt[e] + 1, e)
        self._record(tag, reads, writes)
        return ins

    def dma(self, q, out, in_, reads, writes, **kw):
        i = self.drr[q]
        self.drr[q] = (i + 1) % len(self.dsem[q])
        sem = self.dsem[q][i]
        tot = self.dtot[q][i]
        for d in self._deps(reads, writes):
            self._wait(q, d)
        if tot > 0:
            self._wait(q, (sem, tot, "dma"))
        ins = self.E[q].dma_start(out=out, in_=in_, **kw)
        ins.then_inc(sem, 16)
        tot += 16
        self.dtot[q][i] = tot
        self._record((sem, tot, "dma" + q), reads, writes)

    def barrier(self):
        for e in self.E:
            for f in self.E:
                if f != e and self.cnt[f] > 0:
                    self._wait(e, (self.sem[f], self.cnt[f], f))
            for q in ("sp", "pool"):
                for sem, tot in zip(self.dsem[q], self.dtot[q]):
                    if tot > 0:
                        self._wait(e, (sem, tot, "dma"))

    def finish(self):
        for q in ("sp", "pool"):
            for sem, tot in zip(self.dsem[q], self.dtot[q]):
                if tot > 0:
                    self._wait(q, (sem, tot, "dma"))


class StopBuild(Exception):
    pass


def build(n_layers=2, dbg=None):
    nc = bass.Bass("TRN2", target_bir_lowering=False)
    sc = Sched(nc)
    dr = lambda name, shape, kind="ExternalInput": nc.dram_tensor(name, shape, F32, kind=kind).ap()
    xT_d = dr("xT", [D, S])
    pT_d = dr("pT", [2, 256, S])
    w_in_d = dr("w_in", [2, D, DIN])
    w_out_d = dr("w_out", [2, D, D])
    w_gate_d = dr("w_gate", [2, D, DFF])
    w_up_d = dr("w_up", [2, D, DFF])
    w_down_d = dr("w_down", [2, DFF, D])
    ple_proj_d = dr("ple_proj", [2, 256, D])
    ple_gate_d = dr("ple_gate", [2, D, D])
    pool_w_d = dr("pool_w", [2, 4, 64, 64])
    vec_d = dr("vec", [128, NV])
    yT_d = dr("yT", [D, S], kind="ExternalOutput")
    dbg_d = None
    if dbg is not None:
        dbg_d = dr("dbg", [D, S], kind="ExternalOutput")

    _tiles = []
    _uid = [0]

    def sb(name, shape, dt=F32):
        _uid[0] += 1
        g = nc.sbuf_tensor("t%d_%s" % (_uid[0], name), shape, dt)
        t = g.__enter__()
        _tiles.append(g)
        return t

    def free_tiles(n):
        for _ in range(n):
            _tiles.pop().__exit__(None, None, None)

    PS = [nc.psum_tensor("ps%d" % i, [128, 512], F32).__enter__() for i in range(8)]

    def mm(out, lhsT, rhs, start=True, stop=True, inc=None, extra_reads=()):
        if inc is None:
            inc = stop
        return sc.op("pe", lambda: nc.tensor.matmul(out.ap, lhsT=lhsT.ap, rhs=rhs.ap, start=start, stop=stop),
                     [lhsT, rhs] + list(extra_reads), [out], inc=inc)

    def act(out, in_, func, scale=None, bias=None, extra=()):
        kw = {}
        rd = [in_] + list(extra)
        if scale is not None:
            if isinstance(scale, Ref):
                kw["scale"] = scale.ap
                rd.append(scale)
            else:
                kw["scale"] = float(scale)
        if bias is not None:
            if isinstance(bias, Ref):
                kw["bias"] = bias.ap
                rd.append(bias)
            else:
                kw["bias"] = float(bias)
        return sc.op("act", lambda: nc.scalar.activation(out=out.ap, in_=in_.ap, func=func, **kw), rd, [out])

    def tt(out, a, b, op, eng="dve"):
        e = nc.vector if eng == "dve" else nc.gpsimd
        return sc.op(eng, lambda: e.tensor_tensor(out=out.ap, in0=a.ap, in1=b.ap, op=op), [a, b], [out])

    def ts(out, a, s1, op0, s2=None, op1=None, eng="dve"):
        e = nc.vector if eng == "dve" else nc.gpsimd
        rd = [a]
        v1 = s1
        if isinstance(s1, Ref):
            rd.append(s1)
            v1 = s1.ap
        v2 = s2
        if isinstance(s2, Ref):
            rd.append(s2)
            v2 = s2.ap
        if op1 is None:
            return sc.op(eng, lambda: e.tensor_scalar(out=out.ap, in0=a.ap, scalar1=v1, scalar2=None, op0=op0), rd, [out])
        return sc.op(eng, lambda: e.tensor_scalar(out=out.ap, in0=a.ap, scalar1=v1, scalar2=v2, op0=op0, op1=op1),
                     rd, [out])

    def stt(out, a, s, b, op0, op1):
        rd = [a, b]
        v = s
        if isinstance(s, Ref):
            rd.append(s)
            v = s.ap
        return sc.op("dve", lambda: nc.vector.scalar_tensor_tensor(out=out.ap, in0=a.ap, scalar=v, in1=b.ap,
                                                                   op0=op0, op1=op1), rd, [out])

    def cp(out, in_, eng="dve"):
        if eng == "act":
            return act(out, in_, AF.Copy)
        e = nc.vector if eng == "dve" else nc.gpsimd
        return sc.op(eng, lambda: e.tensor_copy(out=out.ap, in_=in_.ap), [in_], [out])

    def mset(t, val, eng="pool"):
        e = nc.vector if eng == "dve" else nc.gpsimd
        return sc.op(eng, lambda: e.memset(t.ap, val), [], [t])

    xT = sb("xT", [128, 8, S])
    vec = sb("vec", [128, NV])
    ident_f = sb("ident_f", [128, 128])
    ident_b = sb("ident_b", [128, 128], BF16)
    ones_b = sb("ones_b", [128, 128], BF16)
    ones_f = sb("ones_f", [128, 128])
    MsT = sb("MsT", [128, 128])
    Mincl = sb("Mincl", [128, 128])
    Msame = sb("Msame", [128, 128])
    ind = sb("ind", [128, 2])
    invc = sb("invc", [128, 2, 16])

    X = lambda c, t: Ref(xT[:, c, t * TT:(t + 1) * TT], ("xT", c, t))
    VEC = Ref(vec[:, :], "vec")

    def vcol(j, n=1):
        return Ref(vec[:, j:j + n], "vec")

    R_identf = Ref(ident_f[:, :], "ident_f")
    R_identb = Ref(ident_b[:, :], "ident_b")
    R_onesb = Ref(ones_b[:, :], "ones_b")
    R_onesf = Ref(ones_f[:, :], "ones_f")
    R_MsT = Ref(MsT[:, :], "MsT")
    R_Mincl = Ref(Mincl[:, :], "Mincl")
    R_Msame = Ref(Msame[:, :], "Msame")

    sc.dma("sp", vec[:, :], vec_d[:, :], [], [VEC])
    for t in range(NT):
        for c in range(8):
            sc.dma("sp", xT[:, c, t * TT:(t + 1) * TT], xT_d[c * 128:(c + 1) * 128, t * TT:(t + 1) * TT], [], [X(c, t)])

    _skip = os.environ.get('KSKIP', '').split(',')
    mset(R_onesf, 1.0)
    mset(R_onesb, 1.0)
    sc.op("pool", lambda: nc.gpsimd.affine_select(out=ident_f[:, :], in_=ones_f[:, :], pattern=[[1, 128]],
                                                  compare_op=ALU.is_equal, fill=0.0, base=0, channel_multiplier=-1),
          [R_onesf], [R_identf])
    sc.op("pool", lambda: nc.gpsimd.affine_select(out=MsT[:, :], in_=ones_f[:, :], pattern=[[1, 128]],
                                                  compare_op=ALU.is_gt, fill=0.0, base=0, channel_multiplier=-1),
          [R_onesf], [R_MsT])
    if 'msub' not in _skip:
        sc.op("pool", lambda: nc.gpsimd.memset(MsT[0:64, 64:128], 0.0), [], [R_MsT])
    cp(R_identb, R_identf, eng="pool")
    tt(R_Mincl, R_MsT, R_identf, ALU.add, eng="pool")
    NEGb = sb("NEGb", [128, 128], BF16)
    R_NEG = Ref(NEGb[:, :], "NEGb")
    ts(R_NEG, R_Mincl, -1.0, ALU.add, 30000.0, ALU.mult)
    mset(R_Msame, 0.0)
    if 'msub' not in _skip:
        sc.op("pool", lambda: nc.gpsimd.memset(Msame[0:64, 0:64], 1.0), [], [R_Msame])
        sc.op("pool", lambda: nc.gpsimd.memset(Msame[64:128, 64:128], 1.0), [], [R_Msame])
    R_ind = Ref(ind[:, :], "ind")
    mset(R_ind, 0.0)
    if 'msub' not in _skip:
        sc.op("pool", lambda: nc.gpsimd.memset(ind[0:64, 0:1], 1.0), [], [R_ind])
        sc.op("pool", lambda: nc.gpsimd.memset(ind[64:128, 1:2], 1.0), [], [R_ind])
    R_invc = Ref(invc[:, :, :], "invc")
    for ch in range(2):
        sc.op("pool", lambda ch=ch: nc.gpsimd.iota(invc[:, ch, :], pattern=[[1, 16]], base=1, channel_multiplier=0,
                                                  allow_small_or_imprecise_dtypes=True), [], [R_invc])
    for ch in range(2):
        for hf in range(2):
            if 'invc' in _skip:
                continue
            win = (2, 4, 8, 16)[ch * 2 + hf]
            sl = invc[hf * 64:(hf + 1) * 64, ch, :]
            sc.op("dve", lambda sl=sl, win=win: nc.vector.tensor_scalar(out=sl, in0=sl, scalar1=float(win), scalar2=None,
                                                                          op0=ALU.min), [R_invc], [R_invc])
    sc.op("dve", lambda: nc.vector.reciprocal(out=invc[:, :, :], in_=invc[:, :, :]), [R_invc], [R_invc])

    bank_rr = [0]

    def bank():
        i = bank_rr[0]
        bank_rr[0] = (i + 1) % 8
        return Ref(PS[i][:, :], ("ps", i))

    _all_slots = []

    class Slots:
        def __init__(self, name, tiles, ids=None):
            _all_slots.append(self)
            self.name = name
            self.tiles = tiles
            self.free = list(ids if ids is not None else range(len(tiles)))

        def alloc(self):
            if not self.free:
                raise RuntimeError("pool exhausted: " + self.name)
            return self.free.pop(0)

        def release(self, i):
            self.free.append(i)

    pa = Slots("ps", PS, ids=[4, 5, 6, 7] if (N_FILL or BURST_EVERY) else [3, 4, 5, 6, 7])
    pb = Slots("ps", PS, ids=[0, 1, 2])

    def galloc(pool):
        while not pool.free:
            yield "blocked"
        return pool.alloc()

    def _progress():
        return sum(sc.cnt.values()) + sum(sum(v) for v in sc.dtot.values())

    def g_inter(gens, width):
        pending = list(gens)
        active = []
        idle = 0
        while pending or active:
            while pending and len(active) < width:
                active.append(pending.pop(0))
            before = _progress()
            n_act = len(active)
            for g in list(active):
                try:
                    next(g)
                except StopIteration:
                    active.remove(g)
            if _progress() == before and len(active) == n_act:
                idle += 1
                if idle > 50:
                    raise RuntimeError("interleave deadlock (pool starvation): " + str([(p.name, len(p.free)) for p in _all_slots[-12:]]))
            else:
                idle = 0
            yield

    def speedup(g, k):
        while True:
            for _ in range(k):
                try:
                    r = next(g)
                except StopIteration:
                    return
                if r == "blocked":
                    break
            yield

    def run_interleaved(gens, width):
        idle = 0
        it = g_inter(gens, width)
        for _ in it:
            pass

    def _old_run_interleaved(gens, width):
        pending = list(gens)
        active = []
        while pending or active:
            while pending and len(active) < width:
                active.append(pending.pop(0))
            for g in list(active):
                try:
                    next(g)
                except StopIteration:
                    active.remove(g)

    gq_rr = [0]

    def gq(n=1):
        i = gq_rr[0]
        gq_rr[0] = (i + 1) % 4
        b = 4 + i
        return Ref(PS[b][:, 0:n * 128], ("psb", b))

    def sub(ref, ap):
        return Ref(ap, ref.keys)

    fillb = sb("fillb", [128, 512], BF16)
    R_fillb = Ref(fillb[:, :], "fillb")
    mset(R_fillb, 1.0)

    _bc = [0]

    def pe_fill(n=None):
        for _ in range(N_FILL if n is None else n):
            sc.op("pe", lambda: nc.tensor.matmul(PS[3][:, :], lhsT=ones_b[:, :], rhs=fillb[:, :], start=True, stop=True),
                  [R_onesb, R_fillb], [Ref(PS[3][:, :], ("ps", 3))], inc=False)

    class Pool_:
        def __init__(self, name, shape, dt, n):
            self.t = [sb("%s%d" % (name, i), shape, dt) for i in range(n)]
            self.name = name
            self.i = 0
            self.n = n

        def get(self):
            i = self.i
            self.i = (i + 1) % self.n
            return self.t[i], (self.name, i)

    def load_w(dst_tile, key, src_ap):
        sc.dma("pool", dst_tile, src_ap.rearrange("(c p) n -> p c n", p=128), [], [Ref(dst_tile, key)])

    def rstd_from_sumsq(ps_sum, out_rs, scale, tmp_ln):
        act(tmp_ln, ps_sum, AF.Ln, scale=scale, bias=R_eps)
        act(out_rs, tmp_ln, AF.Exp, scale=-0.5)

    eps_t = sb("eps_t", [128, 1])
    R_eps = Ref(eps_t[:, :], "eps")
    mset(R_eps, EPS)
    one_t = sb("one_t", [128, 1])
    R_one = Ref(one_t[:, :], "one")
    mset(R_one, 1.0)

    def sigmoid_to(out, in_, tmp):
        act(tmp, in_, AF.Exp, scale=-1.0)
        act(tmp, tmp, AF.Ln, bias=R_one)
        act(out, tmp, AF.Exp, scale=-1.0)

    n_persist = len(_tiles)

    def checkpoint(name, l):
        if dbg is not None and dbg == (name, l):
            raise StopBuild()

    def _layers(n_layers):
        for l in range(n_layers):
            vb = l * VL
            checkpoint("load", l)
            hT = sb("hT", [128, 8, TT], BF16)
            wstS = Slots("wst", [sb("wst%d" % i, [128, 8, 256], BF16) for i in range(3)])
            wab = sb("wab", [128, 8, 8], BF16)
            pwbd = sb("pwbd", [128, 2, 128], BF16)
            qkvT = sb("qkvT", [128, 12, TT], BF16)
            mixT = sb("mixT", [128, 8, TT], BF16)
            zs = sb("zs", [128, 4, TT], BF16)
            rawS = Slots("raw", [sb("raw%d" % i, [128, TT + 4], BF16) for i in range(3)])
            dgS = Slots("dg", [sb("dg%d" % i, [128, 128], BF16) for i in range(8)])
            accS = Slots("acc", [sb("acc%d" % i, [128, TT], F32) for i in range(2)])
            tmpS = Slots("tmpf", [sb("tmpf%d" % i, [128, TT], F32) for i in range(3)])
            sqS = Slots("sqb", [sb("sqb%d" % i, [128, TT], BF16) for i in range(2)])
            halo = sb("halo", [128, 12, 4], BF16)
            hpb = sb("hpb", [128, 2, TT + 15])
            s2b = sb("s2b", [128, TT + 15])
            s4b = sb("s4b", [128, TT + 15])
            pooledb = sb("pooledb", [128, TT], BF16)
            cbs = sb("cbs", [128, 2, TT], BF16)
            ccs = sb("ccs", [128, 2, TT], BF16)
            ub = sb("ub", [128, 2, TT + 2], BF16)
            colsA = sb("colsA", [128, 10, 16])
            Rl = sb("Rl", [128, 4, 2, 4])
            egl2 = sb("egl", [128, 2, 4, 2, 4])
            g4f = Slots("g4f", [sb("g4f%d" % i, [128, 4, 128], F32) for i in range(N_G4F)])
            g4b = Slots("g4b", [sb("g4b%d" % i, [128, 4, 128], BF16) for i in range(N_G4B)])
            PT = sb("PT", [128, 4, 4, 128], BF16)
            Kd = sb("Kd", [128, 4, 4, 128], BF16)
            WTb = sb("WTb", [128, 4, 4, 128], BF16)
            Ub = sb("Ub", [128, 4, 4, 128], BF16)
            qgT = sb("qgT", [128, 4, TT], BF16)
            Vn = sb("Vn", [128, 4, 128], BF16)
            S32 = sb("S32", [128, 4, 128])
            Sb = [sb("Sb%d" % i, [128, 4, 128], BF16) for i in range(2)]
            oT = sb("oT", [128, 4, TT])
            negA = sb("negA", [128, 16])
            n_mixer = len(_tiles) - n_persist

            R_halo = lambda ch: Ref(halo[:, ch, 0:3], ("halo", ch))
            for ch in range(12):
                mset(R_halo(ch), 0.0)
            R_hpb = lambda ch: Ref(hpb[:, ch, :], ("hpb", ch))
            for ch in range(2):
                sc.op("pool", lambda ch=ch: nc.gpsimd.memset(hpb[:, ch, 0:15], 0.0), [], [R_hpb(ch)])
            R_ub = lambda ch: Ref(ub[:, ch, :], ("ub", ch))
            for ch in range(2):
                sc.op("pool", lambda ch=ch: nc.gpsimd.memset(ub[:, ch, 0:2], 0.0), [], [R_ub(ch)])
            mset(Ref(S32[:, :, :], "S32"), 0.0)
            mset(Ref(Sb[0][:, :, :], ("Sb", 0)), 0.0)
            R_wab = Ref(wab[:, :, :], "wab")
            load_w(wab[:, :, :], "wab", w_in_d[l, :, 2048:2056])
            R_pw = Ref(pwbd[:, :, :], "pwbd")
            mset(R_pw, 0.0)
            for gi in range(4):
                ch, hf = gi // 2, gi % 2
                sc.dma("pool", pwbd[hf * 64:(hf + 1) * 64, ch, hf * 64:(hf + 1) * 64], pool_w_d[l, gi, :, :], [], [R_pw])
            R_negA = Ref(negA[:, :], "negA")
            act(R_negA, vcol(vb + 89, 16), AF.Exp)
            ts(R_negA, R_negA, -1.0, ALU.mult)

            def proj_chunk(wt, wkey, j, tt_):
                ps = bank()
                for c in range(8):
                    mm(ps, Ref(wt[:, c, j * 128:(j + 1) * 128], wkey), Ref(hT[:, c, :], ("hT", c)),
                       start=(c == 0), stop=(c == 7))
                return ps

            def silu_to(out, in_):
                t, k = tmpp.get()
                T = Ref(t[:, :], k)
                sigmoid_to(T, in_, T)
                tt(out, in_, T, ALU.mult)

            def make_tile(tt_):
                CA = lambda q_: Ref(colsA[:, q_, :], ("colsA", q_))
                cav = lambda q_: colsA[:, q_, :].rearrange("p (s h) -> p s h", h=4)
                egl = egl2[:, tt_ % 2, :, :, :]
                eglk = ("egl", tt_ % 2)
                R_egl = Ref(egl, eglk)

                def colsc(q_, m, h):
                    return sub(CA(q_), colsA[:, q_, m * 4 + h:m * 4 + h + 1])

                def g_norm1():
                    ti = yield from galloc(tmpS)
                    sqi = []
                    for _ in range(2):
                        _i = yield from galloc(sqS)
                        sqi.append(_i)
                    b = yield from galloc(pb)
                    ps = Ref(PS[b][:, :], ("ps", b))
                    for c in range(8):
                        SQ = Ref(sqS.tiles[sqi[c % 2]][:, :], ("sqb", sqi[c % 2]))
                        act(SQ, X(c, tt_), AF.Square)
                        mm(ps, R_onesb, SQ, start=(c == 0), stop=(c == 7), inc=True)
                        if c % 2 == 1:
                            yield
                    for i in sqi:
                        sqS.release(i)
                    RS = Ref(tmpS.tiles[ti][:, :], ("tmpf", ti))
                    act(RS, ps, AF.Ln, scale=1.0 / D, bias=R_eps)
                    pb.release(b)
                    act(RS, RS, AF.Exp, scale=-0.5)
                    yield
                    for c in range(8):
                        stt(Ref(hT[:, c, :], ("hT", c)), X(c, tt_), vcol(vb + c), RS, ALU.mult, ALU.mult)
                    tmpS.release(ti)

                F_ = lambda pool, i: Ref(pool.tiles[i][:, :], (pool.name, i))

                def proj_ps(wi, j):
                    b = yield from galloc(pb)
                    ps = Ref(PS[b][:, :], ("ps", b))
                    wt = wstS.tiles[wi]
                    for c in range(8):
                        mm(ps, Ref(wt[:, c, j * 128:(j + 1) * 128], ("wst", wi)), Ref(hT[:, c, :], ("hT", c)),
                           start=(c == 0), stop=(c == 7))
                    proj_cnt[wi] = proj_cnt.get(wi, 0) + 1
                    return b, ps

                def g_sigmoid(T, in_):
                    act(T, in_, AF.Exp, scale=-1.0)
                    yield
                    act(T, T, AF.Ln, bias=R_one)
                    yield
                    act(T, T, AF.Exp, scale=-1.0)
                    yield

                def g_qkv_chunk(wi, j, chn):
                    ri = yield from galloc(rawS)
                    b, ps = yield from proj_ps(wi, j)
                    rt = rawS.tiles[ri]
                    RAW = F_(rawS, ri)
                    cp(sub(RAW, rt[:, 0:3]), R_halo(chn), eng=RAW_ENG)
                    cp(sub(RAW, rt[:, 3:3 + TT]), ps, eng=RAW_ENG)
                    pb.release(b)
                    cp(R_halo(chn), sub(RAW, rt[:, TT:TT + 3]), eng=RAW_ENG)
                    yield
                    cw = vb + 16 + chn * 4
                    dis = []
                    for jj in range(4):
                        di = yield from galloc(dgS)
                        ts(F_(dgS, di), R_identb, vcol(cw + jj), ALU.mult, 0.0, ALU.add, eng="pool")
                        dis.append(di)
                    yield
                    ti = yield from galloc(tmpS)
                    T = F_(tmpS, ti)
                    if chn < 8:
                        ai = yield from galloc(accS)
                        ACC = F_(accS, ai)
                    bc = yield from galloc(pb)
                    psc = Ref(PS[bc][:, :], ("ps", bc))
                    for jj in range(4):
                        mm(psc, F_(dgS, dis[jj]), sub(RAW, rt[:, jj:jj + TT]), start=(jj == 0), stop=(jj == 3))
                    for di in dis:
                        dgS.release(di)
                    rawS.release(ri)
                    yield
                    yield from g_sigmoid(T, psc)
                    dst = Ref(qkvT[:, chn, :], ("qkvT", chn))
                    if chn >= 8:
                        tt(dst, psc, T, ALU.mult)
                        pb.release(bc)
                        tmpS.release(ti)
                        vdone[0] += 1
                        return
                    tt(ACC, psc, T, ALU.mult)
                    pb.release(bc)
                    yield
                    si = yield from galloc(sqS)
                    SQ = F_(sqS, si)
                    if SQ_ENG == "act":
                        act(SQ, ACC, AF.Square)
                    else:
                        tt(SQ, ACC, ACC, ALU.mult)
                    yield
                    b2 = yield from galloc(pb)
                    ps2 = Ref(PS[b2][:, :], ("ps", b2))
                    mm(ps2, R_onesb, SQ)
                    sqS.release(si)
                    yield
                    act(T, ps2, AF.Ln, scale=1.0, bias=R_eps)
                    pb.release(b2)
                    yield
                    act(T, T, AF.Exp, scale=-0.5)
                    yield
                    if chn < 4:
                        stt(dst, ACC, 128.0 ** -0.5, T, ALU.mult, ALU.mult)
                    else:
                        tt(dst, ACC, T, ALU.mult)
                    tmpS.release(ti)
                    accS.release(ai)

                def g_z_chunk(wi, j, h):
                    ti = yield from galloc(tmpS)
                    T = F_(tmpS, ti)
                    b, ps = yield from proj_ps(wi, j)
                    yield from g_sigmoid(T, ps)
                    tt(Ref(zs[:, h, :], ("zs", h)), ps, T, ALU.mult)
                    pb.release(b)
                    tmpS.release(ti)

                def g_pool_chunk(wi, ch):
                    b, ps = yield from proj_ps(wi, ch)
                    HP = R_hpb(ch)
                    cp(sub(HP, hpb[:, ch, 15:15 + TT]), ps, eng="act")
                    pb.release(b)
                    yield
                    n = TT + 15
                    R_s2 = Ref(s2b[:, :], "s2b")
                    R_s4 = Ref(s4b[:, :], "s4b")
                    tt(sub(R_s2, s2b[:, 1:n]), sub(HP, hpb[:, ch, 1:n]), sub(HP, hpb[:, ch, 0:n - 1]), ALU.add)
                    yield
                    tt(sub(R_s4, s4b[:, 3:n]), sub(R_s2, s2b[:, 3:n]), sub(R_s2, s2b[:, 1:n - 2]), ALU.add)
                    yield
                    if ch == 1:
                        tt(sub(R_s2, s2b[:, 7:n]), sub(R_s4, s4b[:, 7:n]), sub(R_s4, s4b[:, 3:n - 4]), ALU.add)
                        yield
                        tt(sub(R_s4, s4b[:, 15:n]), sub(R_s2, s2b[:, 15:n]), sub(R_s2, s2b[:, 7:n - 8]), ALU.add)
                        yield
                    ai = yield from galloc(accS)
                    at = accS.tiles[ai]
                    M = F_(accS, ai)
                    wins = (2, 4) if ch == 0 else (8, 16)
                    for hf in range(2):
                        src_t = s2b if hf == 0 else s4b
                        R_src = R_s2 if hf == 0 else R_s4
                        pp = slice(hf * 64, (hf + 1) * 64)
                        ts(sub(M, at[pp, :]), sub(R_src, src_t[pp, 15:15 + TT]), 1.0 / wins[hf], ALU.mult)
                        if tt_ == 0:
                            tt(sub(M, at[pp, 0:16]), sub(R_src, src_t[pp, 15:31]), sub(R_invc, invc[pp, ch, :]), ALU.mult)
                    yield
                    R_pooled = Ref(pooledb[:, :], "pooledb")
                    tt(R_pooled, M, sub(HP, hpb[:, ch, 15:15 + TT]), ALU.subtract)
                    accS.release(ai)
                    cp(sub(HP, hpb[:, ch, 0:15]), sub(HP, hpb[:, ch, TT:TT + 15]), eng="act")
                    yield
                    b2 = yield from galloc(pb)
                    ps2 = Ref(PS[b2][:, :], ("ps", b2))
                    mm(ps2, sub(R_pw, pwbd[:, ch, :]), R_pooled)
                    yield
                    act(Ref(mixT[:, 4 + ch, :], ("mixT", 4 + ch)), ps2, AF.Copy, scale=vcol(vb + 65 + ch))
                    pb.release(b2)

                def g_store_chunk(wi, ch, dst_t, nm):
                    b, ps = yield from proj_ps(wi, ch)
                    cp(Ref(dst_t[:, ch, :], (nm, ch)), ps, eng="act")
                    pb.release(b)
                    yield

                def g_sconv_chunk(wi, ch):
                    b, ps = yield from proj_ps(wi, ch)
                    U_ = R_ub(ch)
                    tt(sub(U_, ub[:, ch, 2:2 + TT]), Ref(ccs[:, ch, :], ("ccs", ch)), ps, ALU.mult)
                    pb.release(b)
                    yield
                    cw = vb + 67 + ch * 3
                    dis = []
                    for jj in range(3):
                        di = yield from galloc(dgS)
                        ts(F_(dgS, di), R_identb, vcol(cw + jj), ALU.mult, 0.0, ALU.add, eng="pool")
                        dis.append(di)
                    yield
                    bc = yield from galloc(pb)
                    psc = Ref(PS[bc][:, :], ("ps", bc))
                    for jj in range(3):
                        mm(psc, F_(dgS, dis[jj]), sub(U_, ub[:, ch, jj:jj + TT]), start=(jj == 0), stop=(jj == 2))
                    for di in dis:
                        dgS.release(di)
                    yield
                    cp(sub(U_, ub[:, ch, 0:2]), sub(U_, ub[:, ch, TT:TT + 2]), eng="act")
                    tt(Ref(mixT[:, 6 + ch, :], ("mixT", 6 + ch)), psc, Ref(cbs[:, ch, :], ("cbs", ch)), ALU.mult)
                    pb.release(bc)

                blocks = []
                for blk in range(6):
                    blocks.append((blk * 256, lambda wi, j, blk=blk: g_qkv_chunk(wi, j, blk * 2 + j)))
                for blk in range(2):
                    blocks.append((1536 + blk * 256, lambda wi, j, blk=blk: g_z_chunk(wi, j, blk * 2 + j)))
                blocks.append((2056, lambda wi, j: g_pool_chunk(wi, j)))
                blocks.append((2312, lambda wi, j: g_store_chunk(wi, j, cbs, "cbs")))
                blocks.append((2568, lambda wi, j: g_store_chunk(wi, j, ccs, "ccs")))
                blocks.append((2824, lambda wi, j: g_sconv_chunk(wi, j)))
                loaded = {}

                lim = [NA_BLOCKS - 1]
                vdone = [0]

                def issue_loads(upto):
                    upto = min(upto, lim[0])
                    for k in range(len(blocks)):
                        if k > upto:
                            break
                        if k not in loaded and wstS.free:
                            wi = wstS.alloc()
                            load_w(wstS.tiles[wi][:, :, :], ("wst", wi), w_in_d[l, :, blocks[k][0]:blocks[k][0] + 256])
                            loaded[k] = wi

                proj_cnt = {}

                def g_block(k):
                    issue_loads(k + 2)
                    while k not in loaded:
                        issue_loads(k)
                        if k not in loaded:
                            yield "blocked"
                    wi = loaded[k]
                    if PAIR_CHUNKS and k != 8:
                        proj_cnt[wi] = 0
                        released = False
                        for _ in g_inter([blocks[k][1](wi, 0), blocks[k][1](wi, 1)], 2):
                            if not released and proj_cnt[wi] >= 2:
                                wstS.release(wi)
                                released = True
                                issue_loads(k + 3)
                            yield
                        if not released:
                            wstS.release(wi)
                        return
                    g0 = blocks[k][1](wi, 0)
                    yield from g0
                    yield
                    g1 = blocks[k][1](wi, 1)
                    first = True
                    for _ in g1:
                        if first:
                            wstS.release(wi)
                            first = False
                            issue_loads(k + 3)
                        yield
                    if first:
                        wstS.release(wi)

                def g_cols():
                    bAB = yield from galloc(pa)
                    psAB = Ref(PS[bAB][:, 0:128], ("ps", bAB))
                    abv = psAB.ap[:, 0:32].rearrange("p (s e) -> p s e", e=8)
                    for s_ in range(4):
                        for c in range(8):
                            mm(sub(psAB, abv[:, s_, :]), Ref(hT[:, c, s_ * 128:(s_ + 1) * 128], ("hT", c)),
                               sub(R_wab, wab[:, c, :]), start=(c == 0), stop=(c == 7))
                    sc.op("dve", lambda: nc.vector.tensor_tensor(out=cav(5), in0=abv[:, :, 0:4],
                                                                 in1=vec[:, vb + 73:vb + 89].rearrange("p (s h) -> p s h", h=4),
                                                                 op=ALU.add), [psAB, VEC], [CA(5)])
                    act(CA(5), CA(5), AF.Exp)
                    act(CA(5), CA(5), AF.Ln, bias=R_one)
                    tt(CA(0), CA(5), R_negA, ALU.mult)
                    yield
                    sc.op("act", lambda: nc.scalar.activation(out=cav(6), in_=abv[:, :, 4:8], func=AF.Exp, scale=-1.0),
                          [psAB], [CA(6)])
                    act(CA(6), CA(6), AF.Ln, bias=R_one)
                    act(CA(1), CA(6), AF.Exp, scale=-1.0)
                    yield
                    pa.release(bAB)
                    bG = yield from galloc(pa)
                    psG = Ref(PS[bG][:, 0:128], ("ps", bG))
                    gcp = sub(psG, psG.ap[:, 0:16])
                    glp = sub(psG, psG.ap[:, 16:32])
                    eglp = sub(psG, psG.ap[:, 32:64])
                    mm(gcp, R_Mincl, CA(0))
                    mm(glp, R_Msame, CA(0))
                    for j in range(2):
                        sc.op("dve", lambda j=j: nc.vector.tensor_scalar(out=Rl[:, :, j, :], in0=cav(0), scalar1=ind[:, j:j + 1],
                                                                           scalar2=None, op0=ALU.mult),
                              [CA(0), R_ind], [Ref(Rl[:, :, :, :], "Rl")])
                    mm(eglp, R_onesf, Ref(Rl[:, :, :, :].rearrange("p s j h -> p (s j h)"), "Rl"))
                    ts(CA(2), gcp, -1.0, ALU.mult)
                    act(CA(7), gcp, AF.Exp)
                    tt(CA(3), CA(1), CA(7), ALU.mult)
                    yield
                    tt(CA(6), glp, CA(2), ALU.add)
                    act(CA(4), CA(6), AF.Exp)
                    act(Ref(egl2[:, tt_ % 2, :, :, :].rearrange("p s j h -> p (s j h)"), eglk), eglp, AF.Exp)
                    pa.release(bG)


                B4 = lambda t: t[:, :, :]
                MsT_b4 = Ref(MsT[:, :].unsqueeze(1).to_broadcast([128, 4, 128]), "MsT")
                idf_b4 = Ref(ident_f[:, :].unsqueeze(1).to_broadcast([128, 4, 128]), "ident_f")

                def psv(b):
                    return PS[b][:, :].rearrange("p (h t) -> p h t", h=4)

                def mm4(b, lfn, rfn):
                    pe_fill()
                    if BURST_EVERY:
                        _bc[0] += 1
                        if _bc[0] % BURST_EVERY == 0:
                            pe_fill(BURST_N)
                    P_ = Ref(psv(b), ("ps", b))
                    for h in range(4):
                        mm(sub(P_, PS[b][:, h * 128:(h + 1) * 128]), lfn(h), rfn(h))
                    return P_

                done = {}

                def gen_pre(m):
                    cols = slice(m * 128, (m + 1) * 128)
                    qT = lambda h: Ref(qkvT[:, h, cols], ("qkvT", h))
                    kT = lambda h: Ref(qkvT[:, 4 + h, cols], ("qkvT", 4 + h))
                    vT = lambda h: Ref(qkvT[:, 8 + h, cols], ("qkvT", 8 + h))
                    qT4 = Ref(qkvT[:, 0:4, cols], [("qkvT", h) for h in range(4)])
                    csb = lambda q_: Ref(colsA[:, q_, m * 4:(m + 1) * 4].unsqueeze(2).to_broadcast([128, 4, 128]),
                                         ("colsA", q_))
                    f4 = lambda i: Ref(g4f.tiles[i][:, :, :], ("g4f", i))
                    b4 = lambda i: Ref(g4b.tiles[i][:, :, :], ("g4b", i))
                    f4h = lambda i, h: Ref(g4f.tiles[i][:, h, :], ("g4f", i))
                    b4h = lambda i, h: Ref(g4b.tiles[i][:, h, :], ("g4b", i))
                    ig = yield from galloc(g4f)
                    cp(f4(ig), csb(0), eng=BC_ENG)
                    ib = yield from galloc(g4b)
                    cp(b4(ib), csb(1), eng=BC_ENG)
                    ie = yield from galloc(g4b)
                    cp(b4(ie), csb(7), eng=BC_ENG)
                    yield
                    b = yield from galloc(pa)
                    psG_ = Ref(psv(b), ("ps", b))
                    for h in range(4):
                        o_ = sub(psG_, PS[b][:, h * 128:(h + 1) * 128])
                        mm(o_, f4h(ig, h), R_Mincl, start=True, stop=False)
                        mm(o_, R_identb, R_NEG, start=False, stop=True)
                    g4f.release(ig)
                    iE2 = yield from galloc(g4f)
                    for h in range(4):
                        act(f4h(iE2, h), sub(psG_, PS[b][:, h * 128:(h + 1) * 128]), AF.Exp, bias=colsc(2, m, h))
                    pa.release(b)
                    yield
                    b = yield from galloc(pa)
                    psE_ = mm4(b, lambda h: b4h(ie, h), lambda h: R_identb)
                    g4b.release(ie)
                    tt(Ref(qgT[:, :, cols], ("qgT", m)), qT4, psE_, ALU.mult)
                    pa.release(b)
                    yield
                    iGT = yield from galloc(g4f)
                    tt(f4(iGT), f4(iE2), MsT_b4, ALU.mult)
                    b = yield from galloc(pa)
                    psB_ = mm4(b, lambda h: b4h(ib, h), lambda h: R_identb)
                    g4b.release(ib)
                    tt(f4(iGT), f4(iGT), psB_, ALU.mult)
                    pa.release(b)
                    yield
                    b = yield from galloc(pa)
                    psK_ = mm4(b, kT, kT)
                    iB = yield from galloc(g4b)
                    tt(b4(iB), psK_, f4(iGT), ALU.mult)
                    pa.release(b)
                    g4f.release(iGT)
                    iR = yield from galloc(g4b)
                    tt(b4(iR), idf_b4, b4(iB), ALU.subtract)
                    yield
                    b = yield from galloc(pa)
                    psQ_ = mm4(b, kT, qT)
                    tt(Ref(PT[:, :, m, :], ("PT", m)), psQ_, f4(iE2), ALU.mult)
                    pa.release(b)
                    g4f.release(iE2)
                    yield
                    b = yield from galloc(pa)
                    ps_ = mm4(b, lambda h: b4h(iB, h), lambda h: R_identb)
                    iA = yield from galloc(g4b)
                    cp(b4(iA), ps_, eng="act")
                    pa.release(b)
                    yield
                    b = yield from galloc(pa)
                    ps_ = mm4(b, lambda h: b4h(iA, h), lambda h: b4h(iB, h))
                    iBn = yield from galloc(g4b)
                    cp(b4(iBn), ps_, eng=B1_ENG)
                    pa.release(b)
                    yield
                    b = yield from galloc(pa)
                    ps_ = mm4(b, lambda h: b4h(iB, h), lambda h: b4h(iA, h))
                    iAn = yield from galloc(g4b)
                    cp(b4(iAn), ps_, eng="act")
                    pa.release(b)
                    g4b.release(iB)
                    g4b.release(iA)
                    iB, iA = iBn, iAn
                    yield
                    for j in range(1, 6):
                        b = yield from galloc(pa)
                        ps_ = mm4(b, lambda h: b4h(iA, h), lambda h: b4h(iR, h))
                        iRn = yield from galloc(g4b)
                        tt(b4(iRn), b4(iR), ps_, ALU.add)
                        pa.release(b)
                        g4b.release(iR)
                        iR = iRn
                        yield
                        if j < 5:
                            b = yield from galloc(pa)
                            ps_ = mm4(b, lambda h: b4h(iA, h), lambda h: b4h(iB, h))
                            iBn = yield from galloc(g4b)
                            cp(b4(iBn), ps_, eng="act")
                            pa.release(b)
                            yield
                            b = yield from galloc(pa)
                            ps_ = mm4(b, lambda h: b4h(iB, h), lambda h: b4h(iA, h))
                            iAn = yield from galloc(g4b)
                            cp(b4(iAn), ps_, eng="act")
                            pa.release(b)
                            g4b.release(iB)
                            g4b.release(iA)
                            iB, iA = iBn, iAn
                            yield
                    g4b.release(iB)
                    g4b.release(iA)
                    b = yield from galloc(pa)
                    ps_ = mm4(b, kT, lambda h: R_identb)
                    iKT = yield from galloc(g4b)
                    tt(b4(iKT), ps_, csb(3), ALU.mult)
                    tt(Ref(Kd[:, :, m, :], ("Kd", m)), ps_, csb(4), ALU.mult)
                    pa.release(b)
                    yield
                    while vdone[0] < 4:
                        yield "blocked"
                    b = yield from galloc(pa)
                    ps_ = mm4(b, vT, lambda h: R_identb)
                    iVT = yield from galloc(g4b)
                    tt(b4(iVT), ps_, csb(1), ALU.mult)
                    pa.release(b)
                    yield
                    b = yield from galloc(pa)
                    ps_ = mm4(b, lambda h: b4h(iKT, h), lambda h: b4h(iR, h))
                    cp(Ref(WTb[:, :, m, :], ("WTb", m)), ps_, eng="act")
                    pa.release(b)
                    g4b.release(iKT)
                    yield
                    b = yield from galloc(pa)
                    ps_ = mm4(b, lambda h: b4h(iR, h), lambda h: b4h(iVT, h))
                    cp(Ref(Ub[:, :, m, :], ("Ub", m)), ps_, eng=B1_ENG)
                    pa.release(b)
                    g4b.release(iVT)
                    g4b.release(iR)
                    done[m] = True

                def g_scan():
                    R_S32a = Ref(S32[:, :, :], "S32")
                    R_Sba = lambda i: Ref(Sb[i][:, :, :], ("Sb", i))
                    for n in range(8):
                        m, j = n // 2, n % 2
                        while m not in done:
                            yield "blocked"
                        pp = slice(j * 64, (j + 1) * 64)
                        gch = tt_ * 8 + n
                        cur, nxt = gch % 2, (gch + 1) % 2
                        egb = Ref(egl2[:, tt_ % 2, m, j, :].unsqueeze(2).to_broadcast([128, 4, 128]), eglk)
                        R_Vn = Ref(Vn[pp, :, :], "Vn")
                        b1 = yield from galloc(pa)
                        psS = mm4(b1, lambda h: Ref(WTb[:, h, m, :], ("WTb", m)), lambda h: Ref(Sb[cur][:, h, :], ("Sb", cur)))
                        tt(R_Vn, Ref(Ub[pp, :, m, :], ("Ub", m)), sub(psS, psv(b1)[pp, :, :]), ALU.subtract)
                        pa.release(b1)
                        tt(R_S32a, R_S32a, egb, ALU.mult)
                        yield
                        b2 = yield from galloc(pa)
                        psO = Ref(PS[b2][:, 0:256].rearrange("p (h t) -> p h t", h=4), ("ps", b2))
                        for h in range(4):
                            o_ = sub(psO, PS[b2][:, h * 64:(h + 1) * 64])
                            mm(o_, Ref(Sb[cur][:, h, :], ("Sb", cur)), Ref(qgT[:, h, n * 64:(n + 1) * 64], ("qgT", m)),
                               start=True, stop=False)
                            mm(o_, Ref(Vn[pp, h, :], "Vn"), Ref(PT[pp, h, m, j * 64:(j + 1) * 64], ("PT", m)),
                               start=False, stop=True)
                        cp(Ref(oT[:, :, n * 64:(n + 1) * 64], ("oT", n)), psO, eng="act")
                        pa.release(b2)
                        yield
                        b3 = yield from galloc(pa)
                        psU = mm4(b3, lambda h: Ref(Kd[pp, h, m, :], ("Kd", m)), lambda h: Ref(Vn[pp, h, :], "Vn"))
                        tt(R_Sba(nxt), R_S32a, psU, ALU.add)
                        tt(R_S32a, R_S32a, psU, ALU.add)
                        pa.release(b3)
                        yield

                def g_onorm(h):
                    R_o = Ref(oT[:, h, :], [("oT", n) for n in range(8)])
                    ti = yield from galloc(tmpS)
                    si = yield from galloc(sqS)
                    SQ = Ref(sqS.tiles[si][:, :], ("sqb", si))
                    act(SQ, R_o, AF.Square)
                    yield
                    b = yield from galloc(pb)
                    ps2 = Ref(PS[b][:, :], ("ps", b))
                    mm(ps2, R_onesb, SQ)
                    sqS.release(si)
                    yield
                    RS = Ref(tmpS.tiles[ti][:, :], ("tmpf", ti))
                    act(RS, ps2, AF.Ln, scale=1.0 / 128, bias=R_eps)
                    pb.release(b)
                    yield
                    act(RS, RS, AF.Exp, scale=-0.5)
                    yield
                    stt(RS, R_o, vcol(vb + 64), RS, ALU.mult, ALU.mult)
                    yield
                    tt(Ref(mixT[:, h, :], ("mixT", h)), RS, Ref(zs[:, h, :], ("zs", h)), ALU.mult)
                    tmpS.release(ti)

                def dbg_dumps():
                    pass
                    if dbg is not None and dbg == ("qkv", l):
                        for c in range(8):
                            sc.dma("pool", dbg_d[c * 128:(c + 1) * 128, tt_ * TT:(tt_ + 1) * TT], qkvT[:, c, :],
                                   [Ref(qkvT[:, c, :], ("qkvT", c))], [])
                    if dbg is not None and dbg == ("o", l):
                        for c in range(4):
                            sc.dma("sp", dbg_d[c * 128:(c + 1) * 128, tt_ * TT:(tt_ + 1) * TT], oT[:, c, :],
                                   [Ref(oT[:, c, :], [("oT", n) for n in range(8)])], [])
                    if dbg is not None and dbg == ("mixed", l):
                        for c in range(8):
                            sc.dma("pool", dbg_d[c * 128:(c + 1) * 128, tt_ * TT:(tt_ + 1) * TT], mixT[:, c, :],
                                   [Ref(mixT[:, c, :], ("mixT", c))], [])
                def g_wout():
                    for blk in range(4):
                        wi = yield from galloc(wstS)
                        wt = wstS.tiles[wi]
                        load_w(wt[:, :, :], ("wst", wi), w_out_d[l, :, blk * 256:(blk + 1) * 256])
                        bs = []
                        for j in range(2):
                            b = yield from galloc(pb)
                            bs.append(b)
                        for j in range(2):
                            ps = Ref(PS[bs[j]][:, :], ("ps", bs[j]))
                            for c in range(8):
                                mm(ps, Ref(wt[:, c, j * 128:(j + 1) * 128], ("wst", wi)), Ref(mixT[:, c, :], ("mixT", c)),
                                   start=(c == 0), stop=(c == 7))
                        wstS.release(wi)
                        yield
                        for j in range(2):
                            o_ = blk * 2 + j
                            ps = Ref(PS[bs[j]][:, :], ("ps", bs[j]))
                            tt(X(o_, tt_), X(o_, tt_), ps, ALU.add)
                            pb.release(bs[j])
                        yield

                def gA():
                    yield from g_norm1()
                    if COLS_IN_A:
                        yield from g_inter([g_cols()] + [g_block(k) for k in range(NA_BLOCKS)], CHUNK_WIDTH + 1)
                    else:
                        yield from g_inter([g_block(k) for k in range(NA_BLOCKS)], CHUNK_WIDTH)

                def g_gdn_pre():
                    if not COLS_IN_A:
                        yield from g_cols()
                    yield from g_inter([gen_pre(m) for m in range(4)], PRE_WIDTH)

                def gB():
                    lim[0] = len(blocks) - 1
                    gl = [g_gdn_pre()]
                    if SCAN_IN_B:
                        gl.append(g_scan())
                    yield from g_inter(gl + [g_block(k) for k in range(NA_BLOCKS, len(blocks))], len(gl) + 2)

                def gC():
                    if not SCAN_IN_B:
                        yield from g_scan()
                    yield from g_inter([g_onorm(h) for h in range(4)], 2)
                    dbg_dumps()
                    yield from g_wout()

                return gA, gB, gC

            tiles_ = [make_tile(t) for t in range(NT)]
            run_interleaved([tiles_[0][0]()], 1)
            run_interleaved([tiles_[0][1]()], 1)
            for t in range(NT):
                if t + 1 < NT:
                    run_interleaved([speedup(tiles_[t][2](), C_SPEED), speedup(tiles_[t + 1][0](), A_SPEED)], 2)
                    run_interleaved([tiles_[t + 1][1]()], 1)
                else:
                    run_interleaved([tiles_[t][2]()], 1)

            sc.barrier()
            free_tiles(n_mixer)
            checkpoint("x1", l)
            checkpoint("mixed", l)
            checkpoint("qkv", l)
            checkpoint("o", l)

            h2T = sb("h2T", [128, 8, S], BF16)
            ffp = Pool_("ffT", [128, 2, S], BF16, 2)
            wgp = Pool_("wg", [128, 8, 256], BF16, 2)
            wup = Pool_("wu", [128, 8, 256], BF16, 2)
            wdp = Pool_("wd", [128, 2, D], BF16, 2)
            tmpp = Pool_("tmpf", [128, TT], F32, 4)
            sqp = Pool_("sqb", [128, TT], BF16, 2)
            pTb = sb("pTb", [128, 2, S], BF16)
            ppw = sb("ppw", [128, 2, D], BF16)
            pgp = Pool_("pg", [128, 8, 256], BF16, 2)
            n_ffn = len(_tiles) - n_persist
            H2 = lambda c, t: Ref(h2T[:, c, t * TT:(t + 1) * TT], ("h2T", c, t))

            for c2 in range(2):
                for hf in range(2):
                    sc.dma("pool", pTb[:, c2, hf * 1024:(hf + 1) * 1024],
                           pT_d[l, c2 * 128:(c2 + 1) * 128, hf * 1024:(hf + 1) * 1024], [], [Ref(pTb[:, c2, :], ("pTb", c2, hf))])
            load_w(ppw[:, :, :], "ppw", ple_proj_d[l, :, :])

            for tt_ in range(NT):
                ps = bank()
                for c in range(8):
                    t, k = sqp.get()
                    SQ = Ref(t[:, :], k)
                    act(SQ, X(c, tt_), AF.Square)
                    mm(ps, R_onesb, SQ, start=(c == 0), stop=(c == 7), inc=True)
                t, k = tmpp.get()
                LN = Ref(t[:, :], k)
                t, k = tmpp.get()
                RS = Ref(t[:, :], k)
                rstd_from_sumsq(ps, RS, 1.0 / D, LN)
                for c in range(8):
                    stt(H2(c, tt_), X(c, tt_), vcol(vb + 8 + c), RS, ALU.mult, ALU.mult)

            for grp in range(NF // 2):
                f0 = grp * 256
                wg, wgk = wgp.get()
                load_w(wg[:, :, :], wgk, w_gate_d[l, :, f0:f0 + 256])
                wu, wuk = wup.get()
                load_w(wu[:, :, :], wuk, w_up_d[l, :, f0:f0 + 256])
                wd, wdk = wdp.get()
                load_w(wd[:, :, :], wdk, w_down_d[l, f0:f0 + 256, :])
                ff, ffk = ffp.get()
                FF = lambda fi, t: Ref(ff[:, fi, t * TT:(t + 1) * TT], (ffk, fi, t))
                for fi in range(2):
                    for tt_ in range(NT):
                        psg = bank()
                        for c in range(8):
                            mm(psg, Ref(wg[:, c, fi * 128:(fi + 1) * 128], wgk), H2(c, tt_), start=(c == 0), stop=(c == 7))
                        psu = bank()
                        for c in range(8):
                            mm(psu, Ref(wu[:, c, fi * 128:(fi + 1) * 128], wuk), H2(c, tt_), start=(c == 0), stop=(c == 7))
                        t, k = tmpp.get()
                        SG = Ref(t[:, :], k)
                        sigmoid_to(SG, psg, SG)
                        tt(SG, SG, psg, ALU.mult)
                        tt(FF(fi, tt_), SG, psu, ALU.mult)
                for o_ in range(8):
                    for tt_ in range(NT):
                        ps = bank()
                        for fi in range(2):
                            mm(ps, Ref(wd[:, fi, o_ * 128:(o_ + 1) * 128], wdk), FF(fi, tt_), start=(fi == 0), stop=(fi == 1))
                        tt(X(o_, tt_), X(o_, tt_), ps, ALU.add)

            checkpoint("x2", l)
            for c in range(8):
                for tt_ in range(NT):
                    cp(H2(c, tt_), X(c, tt_), eng=("act" if (c + tt_) % 2 else "dve"))
            for blk in range(4):
                pg, pgk = pgp.get()
                load_w(pg[:, :, :], pgk, ple_gate_d[l, :, blk * 256:(blk + 1) * 256])
                for j in range(2):
                    o_ = blk * 2 + j
                    for tt_ in range(NT):
                        psg = bank()
                        for c in range(8):
                            mm(psg, Ref(pg[:, c, j * 128:(j + 1) * 128], pgk), H2(c, tt_), start=(c == 0), stop=(c == 7))
                        psp = bank()
                        for c2 in range(2):
                            mm(psp, Ref(ppw[:, c2, o_ * 128:(o_ + 1) * 128], "ppw"),
                               Ref(pTb[:, c2, tt_ * TT:(tt_ + 1) * TT], ("pTb", c2, tt_ // 2)), start=(c2 == 0), stop=(c2 == 1))
                        t, k = tmpp.get()
                        SG = Ref(t[:, :], k)
                        sigmoid_to(SG, psg, SG)
                        tt(SG, SG, psp, ALU.mult)
                        tt(X(o_, tt_), X(o_, tt_), SG, ALU.add)
            sc.barrier()
            free_tiles(n_ffn)


    def _final():
        if 'final' in _skip:
            for c in range(8):
                for t in range(NT):
                    sc.dma("sp", yT_d[c * 128:(c + 1) * 128, t * TT:(t + 1) * TT], xT[:, c, t * TT:(t + 1) * TT], [X(c, t)], [])
            return
        tmpp = Pool_("tmpf", [128, TT], F32, 4)
        sqp = Pool_("sqb", [128, TT], BF16, 2)
        outp = Pool_("outb", [128, TT], F32, 4)
        if dbg is not None and dbg[0] not in ("mixed", "qkv", "o"):
            for c in range(8):
                for t in range(NT):
                    sc.dma("sp", dbg_d[c * 128:(c + 1) * 128, t * TT:(t + 1) * TT], xT[:, c, t * TT:(t + 1) * TT], [X(c, t)], [])
        for tt_ in range(NT):
            ps = bank()
            for c in range(8):
                t, k = sqp.get()
                SQ = Ref(t[:, :], k)
                act(SQ, X(c, tt_), AF.Square)
                mm(ps, R_onesb, SQ, start=(c == 0), stop=(c == 7), inc=True)
            t, k = tmpp.get()
            LN = Ref(t[:, :], k)
            t, k = tmpp.get()
            RS = Ref(t[:, :], k)
            rstd_from_sumsq(ps, RS, 1.0 / D, LN)
            for c in range(8):
                t, k = outp.get()
                OB = Ref(t[:, :], k)
                stt(OB, X(c, tt_), vcol(2 * VL + c), RS, ALU.mult, ALU.mult)
                sc.dma("sp", yT_d[c * 128:(c + 1) * 128, tt_ * TT:(tt_ + 1) * TT], t[:, :], [OB], [])
    try:
        _layers(n_layers)
    except StopBuild:
        sc.barrier()
        free_tiles(len(_tiles) - n_persist)
    _final()
    sc.finish()
    return nc


def pack_vec(inp):
    vec = np.zeros((128, NV), np.float32)
    for l in range(2):
        b = l * VL
        vec[:, b:b + 8] = inp["norm1_g"][l].reshape(8, 128).T
        vec[:, b + 8:b + 16] = inp["norm2_g"][l].reshape(8, 128).T
        cq = inp["conv_qkv"][l]
        vec[:, b + 16:b + 64] = cq.reshape(4, 12, 128).transpose(2, 1, 0).reshape(128, 48)
        vec[:, b + 64] = inp["onorm_g"][l]
        vec[:, b + 65:b + 67] = inp["pool_scale"][l].reshape(2, 128).T
        sw = inp["sconv_w"][l]
        vec[:, b + 67:b + 73] = sw.reshape(3, 2, 128).transpose(2, 1, 0).reshape(128, 6)
        vec[:, b + 73:b + 89] = np.broadcast_to(np.tile(inp["dt_bias"][l], 4)[None, :], (128, 16))
        vec[:, b + 89:b + 105] = np.broadcast_to(np.tile(inp["a_log"][l], 4)[None, :], (128, 16))
    vec[:, 2 * VL:2 * VL + 8] = inp["final_g"].reshape(8, 128).T
    return vec


_NC_CACHE = {}


def kernel(**inputs):
    inp = {k: np.asarray(v) for k, v in inputs.items()}
    x = inp["x"].astype(np.float32, copy=False)
    p = inp["p"].astype(np.float32, copy=False)
    vec = pack_vec(inp)
    shared = {
        "w_in": np.ascontiguousarray(inp["w_in"], np.float32),
        "w_out": np.ascontiguousarray(inp["w_out"], np.float32),
        "w_gate": np.ascontiguousarray(inp["w_gate"], np.float32),
        "w_up": np.ascontiguousarray(inp["w_up"], np.float32),
        "w_down": np.ascontiguousarray(inp["w_down"], np.float32),
        "ple_proj": np.ascontiguousarray(inp["ple_proj"], np.float32),
        "ple_gate": np.ascontiguousarray(inp["ple_gate"], np.float32),
        "pool_w": np.ascontiguousarray(inp["pool_w"], np.float32),
        "vec": vec,
    }
    in_maps = []
    for b in range(N_CORES):
        m = dict(shared)
        m["xT"] = np.ascontiguousarray(x[b].T)
        m["pT"] = np.ascontiguousarray(p[:, b].transpose(0, 2, 1))
        in_maps.append(m)
    if "nc" not in _NC_CACHE:
        _NC_CACHE["nc"] = build()
    res = run_bass_kernel_spmd(_NC_CACHE["nc"], in_maps, core_ids=list(range(N_CORES)))
    out = np.stack([np.ascontiguousarray(r["yT"].T) for r in res.results], axis=0)
    return out.astype(np.float32, copy=False)
```

```python
import os
import numpy as np
import concourse.bass as bass
import concourse.mybir as mybir
from concourse.bass_utils import run_bass_kernel_spmd

F32 = mybir.dt.float32
BF16 = mybir.dt.bfloat16
AF = mybir.ActivationFunctionType
ALU = mybir.AluOpType

D = 1024
S = 2048
TT = 512
NT = S // TT
DIN = 3080
DFF = 2816
NF = DFF // 128
EPS = 1e-6
VL = 105
NV = 2 * VL + 8
N_CORES = 8
PRE_WIDTH = int(os.environ.get('K_PRE_WIDTH', '2'))
N_FILL = int(os.environ.get('K_FILL', '0'))
BURST_EVERY = int(os.environ.get('K_BURST_EVERY', '0'))
BURST_N = int(os.environ.get('K_BURST_N', '16'))
COLS_IN_A = bool(int(os.environ.get('K_COLS_IN_A', '1')))
SCAN_IN_B = bool(int(os.environ.get('K_SCAN_IN_B', '0')))
CHUNK_WIDTH = int(os.environ.get('K_CHUNK_WIDTH', '2'))
RAW_ENG = os.environ.get('K_RAW_ENG', 'dve')
SQ_ENG = os.environ.get('K_SQ_ENG', 'act')
B1_ENG = os.environ.get('K_B1_ENG', 'act')
BC_ENG = os.environ.get('K_BC_ENG', 'dve')
C_SPEED = int(os.environ.get('K_C_SPEED', '1'))
B_EXTRA = int(os.environ.get('K_B_EXTRA', '2'))
PRE_SPEED = int(os.environ.get('K_PRE_SPEED', '1'))
A_SPEED = int(os.environ.get('K_A_SPEED', '2'))
Z_IN_A = bool(int(os.environ.get('K_Z_IN_A', '0')))
NA_BLOCKS = int(os.environ.get('K_NA_BLOCKS', '6' if Z_IN_A else '4'))
PAIR_CHUNKS = bool(int(os.environ.get('K_PAIR', '1')))
N_G4F = 6
N_G4B = 15
SAME_ENGINE_NOWAIT = bool(int(os.environ.get('K_SE_NOWAIT', '0')))


class Ref:
    __slots__ = ("ap", "keys")

    def __init__(self, ap, keys):
        self.ap = ap
        self.keys = keys if isinstance(keys, list) else [keys]


class Sched:
    def __init__(self, nc, n_dma_sems=8):
        self.nc = nc
        self.E = {"pe": nc.tensor, "act": nc.scalar, "dve": nc.vector, "pool": nc.gpsimd, "sp": nc.sync}
        self.sem = {}
        self.cnt = {}
        for e in self.E:
            self.sem[e] = nc.semaphore("s_" + e).__enter__()
            self.cnt[e] = 0
        self.seen = {e: {} for e in self.E}
        self.lastw = {}
        self.readers = {}
        self.dsem = {}
        self.dtot = {}
        self.drr = {}
        for q in ("sp", "pool"):
            self.dsem[q] = [nc.semaphore("d_%s%d" % (q, i)).__enter__() for i in range(n_dma_sems)]
            self.dtot[q] = [0] * n_dma_sems
            self.drr[q] = 0

    def _wait(self, e, dep):
        sem, val, src = dep
        if src == e and (e == "pe" or (SAME_ENGINE_NOWAIT and e in ("act", "dve"))):
            return
        k = id(sem)
        if self.seen[e].get(k, 0) >= val:
            return
        self.E[e].wait_ge(sem, val)
        self.seen[e][k] = val

    def _deps(self, reads, writes):
        deps = []
        for r in reads:
            for k in r.keys:
                if k in self.lastw:
                    deps.append(self.lastw[k])
        for w in writes:
            for k in w.keys:
                if k in self.lastw:
                    deps.append(self.lastw[k])
                rd = self.readers.get(k)
                if rd:
                    deps.extend(rd.values())
        return deps

    def _record(self, tag, reads, writes):
        for w in writes:
            for k in w.keys:
                self.lastw[k] = tag
                self.readers[k] = {}
        for r in reads:
            for k in r.keys:
                self.readers.setdefault(k, {})[id(tag[0])] = tag

    def op(self, e, fn, reads, writes, inc=True):
        for d in self._deps(reads, writes):
            self._wait(e, d)
        ins = fn()
        if inc:
            ins.then_inc(self.sem[e], 1)
            self.cnt[e] += 1
            tag = (self.sem[e], self.cnt[e], e)
        else:
            tag = (self.sem[e], self.cnt[e] + 1, e)
        self._record(tag, reads, writes)
        return ins

    def dma(self, q, out, in_, reads, writes, **kw):
        i = self.drr[q]
        self.drr[q] = (i + 1) % len(self.dsem[q])
        sem = self.dsem[q][i]
        tot = self.dtot[q][i]
        for d in self._deps(reads, writes):
            self._wait(q, d)
        if tot > 0:
            self._wait(q, (sem, tot, "dma"))
        ins = self.E[q].dma_start(out=out, in_=in_, **kw)
        ins.then_inc(sem, 16)
        tot += 16
        self.dtot[q][i] = tot
        self._record((sem, tot, "dma" + q), reads, writes)

    def barrier(self):
        for e in self.E:
            for f in self.E:
                if f != e and self.cnt[f] > 0:
                    self._wait(e, (self.sem[f], self.cnt[f], f))
            for q in ("sp", "pool"):
                for sem, tot in zip(self.dsem[q], self.dtot[q]):
                    if tot > 0:
                        self._wait(e, (sem, tot, "dma"))

    def finish(self):
        for q in ("sp", "pool"):
            for sem, tot in zip(self.dsem[q], self.dtot[q]):
                if tot > 0:
                    self._wait(q, (sem, tot, "dma"))


class StopBuild(Exception):
    pass


def build(n_layers=2, dbg=None):
    nc = bass.Bass("TRN2", target_bir_lowering=False)
    sc = Sched(nc)
    dr = lambda name, shape, kind="ExternalInput": nc.dram_tensor(name, shape, F32, kind=kind).ap()
    xT_d = dr("xT", [D, S])
    pT_d = dr("pT", [2, 256, S])
    w_in_d = dr("w_in", [2, D, DIN])
    w_out_d = dr("w_out", [2, D, D])
    w_gate_d = dr("w_gate", [2, D, DFF])
    w_up_d = dr("w_up", [2, D, DFF])
    w_down_d = dr("w_down", [2, DFF, D])
    ple_proj_d = dr("ple_proj", [2, 256, D])
    ple_gate_d = dr("ple_gate", [2, D, D])
    pool_w_d = dr("pool_w", [2, 4, 64, 64])
    vec_d = dr("vec", [128, NV])
    yT_d = dr("yT", [D, S], kind="ExternalOutput")
    dbg_d = None
    if dbg is not None:
        dbg_d = dr("dbg", [D, S], kind="ExternalOutput")

    _tiles = []
    _uid = [0]

    def sb(name, shape, dt=F32):
        _uid[0] += 1
        g = nc.sbuf_tensor("t%d_%s" % (_uid[0], name), shape, dt)
        t = g.__enter__()
        _tiles.append(g)
        return t

    def free_tiles(n):
        for _ in range(n):
            _tiles.pop().__exit__(None, None, None)

    PS = [nc.psum_tensor("ps%d" % i, [128, 512], F32).__enter__() for i in range(8)]

    def mm(out, lhsT, rhs, start=True, stop=True, inc=None, extra_reads=()):
        if inc is None:
            inc = stop
        return sc.op("pe", lambda: nc.tensor.matmul(out.ap, lhsT=lhsT.ap, rhs=rhs.ap, start=start, stop=stop),
                     [lhsT, rhs] + list(extra_reads), [out], inc=inc)

    def act(out, in_, func, scale=None, bias=None, extra=()):
        kw = {}
        rd = [in_] + list(extra)
        if scale is not None:
            if isinstance(scale, Ref):
                kw["scale"] = scale.ap
                rd.append(scale)
            else:
                kw["scale"] = float(scale)
        if bias is not None:
            if isinstance(bias, Ref):
                kw["bias"] = bias.ap
                rd.append(bias)
            else:
                kw["bias"] = float(bias)
        return sc.op("act", lambda: nc.scalar.activation(out=out.ap, in_=in_.ap, func=func, **kw), rd, [out])

    def tt(out, a, b, op, eng="dve"):
        e = nc.vector if eng == "dve" else nc.gpsimd
        return sc.op(eng, lambda: e.tensor_tensor(out=out.ap, in0=a.ap, in1=b.ap, op=op), [a, b], [out])

    def ts(out, a, s1, op0, s2=None, op1=None, eng="dve"):
        e = nc.vector if eng == "dve" else nc.gpsimd
        rd = [a]
        v1 = s1
        if isinstance(s1, Ref):
            rd.append(s1)
            v1 = s1.ap
        v2 = s2
        if isinstance(s2, Ref):
            rd.append(s2)
            v2 = s2.ap
        if op1 is None:
            return sc.op(eng, lambda: e.tensor_scalar(out=out.ap, in0=a.ap, scalar1=v1, scalar2=None, op0=op0), rd, [out])
        return sc.op(eng, lambda: e.tensor_scalar(out=out.ap, in0=a.ap, scalar1=v1, scalar2=v2, op0=op0, op1=op1),
                     rd, [out])

    def stt(out, a, s, b, op0, op1):
        rd = [a, b]
        v = s
        if isinstance(s, Ref):
            rd.append(s)
            v = s.ap
        return sc.op("dve", lambda: nc.vector.scalar_tensor_tensor(out=out.ap, in0=a.ap, scalar=v, in1=b.ap,
                                                                   op0=op0, op1=op1), rd, [out])

    def cp(out, in_, eng="dve"):
        if eng == "act":
            return act(out, in_, AF.Copy)
        e = nc.vector if eng == "dve" else nc.gpsimd
        return sc.op(eng, lambda: e.tensor_copy(out=out.ap, in_=in_.ap), [in_], [out])

    def mset(t, val, eng="pool"):
        e = nc.vector if eng == "dve" else nc.gpsimd
        return sc.op(eng, lambda: e.memset(t.ap, val), [], [t])

    xT = sb("xT", [128, 8, S])
    vec = sb("vec", [128, NV])
    ident_f = sb("ident_f", [128, 128])
    ident_b = sb("ident_b", [128, 128], BF16)
    ones_b = sb("ones_b", [128, 128], BF16)
    ones_f = sb("ones_f", [128, 128])
    MsT = sb("MsT", [128, 128])
    Mincl = sb("Mincl", [128, 128])
    Msame = sb("Msame", [128, 128])
    ind = sb("ind", [128, 2])
    invc = sb("invc", [128, 2, 16])

    X = lambda c, t: Ref(xT[:, c, t * TT:(t + 1) * TT], ("xT", c, t))
    VEC = Ref(vec[:, :], "vec")

    def vcol(j, n=1):
        return Ref(vec[:, j:j + n], "vec")

    R_identf = Ref(ident_f[:, :], "ident_f")
    R_identb = Ref(ident_b[:, :], "ident_b")
    R_onesb = Ref(ones_b[:, :], "ones_b")
    R_onesf = Ref(ones_f[:, :], "ones_f")
    R_MsT = Ref(MsT[:, :], "MsT")
    R_Mincl = Ref(Mincl[:, :], "Mincl")
    R_Msame = Ref(Msame[:, :], "Msame")

    sc.dma("sp", vec[:, :], vec_d[:, :], [], [VEC])
    for t in range(NT):
        for c in range(8):
            sc.dma("sp", xT[:, c, t * TT:(t + 1) * TT], xT_d[c * 128:(c + 1) * 128, t * TT:(t + 1) * TT], [], [X(c, t)])

    _skip = os.environ.get('KSKIP', '').split(',')
    mset(R_onesf, 1.0)
    mset(R_onesb, 1.0)
    sc.op("pool", lambda: nc.gpsimd.affine_select(out=ident_f[:, :], in_=ones_f[:, :], pattern=[[1, 128]],
                                                  compare_op=ALU.is_equal, fill=0.0, base=0, channel_multiplier=-1),
          [R_onesf], [R_identf])
    sc.op("pool", lambda: nc.gpsimd.affine_select(out=MsT[:, :], in_=ones_f[:, :], pattern=[[1, 128]],
                                                  compare_op=ALU.is_gt, fill=0.0, base=0, channel_multiplier=-1),
          [R_onesf], [R_MsT])
    if 'msub' not in _skip:
        sc.op("pool", lambda: nc.gpsimd.memset(MsT[0:64, 64:128], 0.0), [], [R_MsT])
    cp(R_identb, R_identf, eng="pool")
    tt(R_Mincl, R_MsT, R_identf, ALU.add, eng="pool")
    NEGb = sb("NEGb", [128, 128], BF16)
    R_NEG = Ref(NEGb[:, :], "NEGb")
    ts(R_NEG, R_Mincl, -1.0, ALU.add, 30000.0, ALU.mult)
    mset(R_Msame, 0.0)
    if 'msub' not in _skip:
        sc.op("pool", lambda: nc.gpsimd.memset(Msame[0:64, 0:64], 1.0), [], [R_Msame])
        sc.op("pool", lambda: nc.gpsimd.memset(Msame[64:128, 64:128], 1.0), [], [R_Msame])
    R_ind = Ref(ind[:, :], "ind")
    mset(R_ind, 0.0)
    if 'msub' not in _skip:
        sc.op("pool", lambda: nc.gpsimd.memset(ind[0:64, 0:1], 1.0), [], [R_ind])
        sc.op("pool", lambda: nc.gpsimd.memset(ind[64:128, 1:2], 1.0), [], [R_ind])
    R_invc = Ref(invc[:, :, :], "invc")
    for ch in range(2):
        sc.op("pool", lambda ch=ch: nc.gpsimd.iota(invc[:, ch, :], pattern=[[1, 16]], base=1, channel_multiplier=0,
                                                  allow_small_or_imprecise_dtypes=True), [], [R_invc])
    for ch in range(2):
        for hf in range(2):
            if 'invc' in _skip:
                continue
            win = (2, 4, 8, 16)[ch * 2 + hf]
            sl = invc[hf * 64:(hf + 1) * 64, ch, :]
            sc.op("dve", lambda sl=sl, win=win: nc.vector.tensor_scalar(out=sl, in0=sl, scalar1=float(win), scalar2=None,
                                                                          op0=ALU.min), [R_invc], [R_invc])
    sc.op("dve", lambda: nc.vector.reciprocal(out=invc[:, :, :], in_=invc[:, :, :]), [R_invc], [R_invc])

    bank_rr = [0]

    def bank():
        i = bank_rr[0]
        bank_rr[0] = (i + 1) % 8
        return Ref(PS[i][:, :], ("ps", i))

    _all_slots = []

    class Slots:
        def __init__(self, name, tiles, ids=None):
            _all_slots.append(self)
            self.name = name
            self.tiles = tiles
            self.free = list(ids if ids is not None else range(len(tiles)))

        def alloc(self):
            if not self.free:
                raise RuntimeError("pool exhausted: " + self.name)
            return self.free.pop(0)

        def release(self, i):
            self.free.append(i)

    pa = Slots("ps", PS, ids=[4, 5, 6, 7] if (N_FILL or BURST_EVERY) else [3, 4, 5, 6, 7])
    pb = Slots("ps", PS, ids=[0, 1, 2])

    def galloc(pool):
        while not pool.free:
            yield "blocked"
        return pool.alloc()

    def _progress():
        return sum(sc.cnt.values()) + sum(sum(v) for v in sc.dtot.values())

    def g_inter(gens, width):
        pending = list(gens)
        active = []
        idle = 0
        while pending or active:
            while pending and len(active) < width:
                active.append(pending.pop(0))
            before = _progress()
            n_act = len(active)
            for g in list(active):
                try:
                    next(g)
                except StopIteration:
                    active.remove(g)
            if _progress() == before and len(active) == n_act:
                idle += 1
                if idle > 50:
                    raise RuntimeError("interleave deadlock (pool starvation): " + str([(p.name, len(p.free)) for p in _all_slots[-12:]]))
            else:
                idle = 0
            yield

    def speedup(g, k):
        while True:
            for _ in range(k):
                try:
                    r = next(g)
                except StopIteration:
                    return
                if r == "blocked":
                    break
            yield

    def run_interleaved(gens, width):
        idle = 0
        it = g_inter(gens, width)
        for _ in it:
            pass

    def _old_run_interleaved(gens, width):
        pending = list(gens)
        active = []
        while pending or active:
            while pending and len(active) < width:
                active.append(pending.pop(0))
            for g in list(active):
                try:
                    next(g)
                except StopIteration:
                    active.remove(g)

    gq_rr = [0]

    def gq(n=1):
        i = gq_rr[0]
        gq_rr[0] = (i + 1) % 4
        b = 4 + i
        return Ref(PS[b][:, 0:n * 128], ("psb", b))

    def sub(ref, ap):
        return Ref(ap, ref.keys)

    fillb = sb("fillb", [128, 512], BF16)
    R_fillb = Ref(fillb[:, :], "fillb")
    mset(R_fillb, 1.0)

    _bc = [0]

    def pe_fill(n=None):
        for _ in range(N_FILL if n is None else n):
            sc.op("pe", lambda: nc.tensor.matmul(PS[3][:, :], lhsT=ones_b[:, :], rhs=fillb[:, :], start=True, stop=True),
                  [R_onesb, R_fillb], [Ref(PS[3][:, :], ("ps", 3))], inc=False)

    class Pool_:
        def __init__(self, name, shape, dt, n):
            self.t = [sb("%s%d" % (name, i), shape, dt) for i in range(n)]
            self.name = name
            self.i = 0
            self.n = n

        def get(self):
            i = self.i
            self.i = (i + 1) % self.n
            return self.t[i], (self.name, i)

    def load_w(dst_tile, key, src_ap):
        sc.dma("pool", dst_tile, src_ap.rearrange("(c p) n -> p c n", p=128), [], [Ref(dst_tile, key)])

    def rstd_from_sumsq(ps_sum, out_rs, scale, tmp_ln):
        act(tmp_ln, ps_sum, AF.Ln, scale=scale, bias=R_eps)
        act(out_rs, tmp_ln, AF.Exp, scale=-0.5)

    eps_t = sb("eps_t", [128, 1])
    R_eps = Ref(eps_t[:, :], "eps")
    mset(R_eps, EPS)
    one_t = sb("one_t", [128, 1])
    R_one = Ref(one_t[:, :], "one")
    mset(R_one, 1.0)

    def sigmoid_to(out, in_, tmp):
        act(tmp, in_, AF.Exp, scale=-1.0)
        act(tmp, tmp, AF.Ln, bias=R_one)
        act(out, tmp, AF.Exp, scale=-1.0)

    n_persist = len(_tiles)

    def checkpoint(name, l):
        if dbg is not None and dbg == (name, l):
            raise StopBuild()

    def _layers(n_layers):
        for l in range(n_layers):
            vb = l * VL
            checkpoint("load", l)
            hT = sb("hT", [128, 8, TT], BF16)
            wstS = Slots("wst", [sb("wst%d" % i, [128, 8, 256], BF16) for i in range(3)])
            wab = sb("wab", [128, 8, 8], BF16)
            pwbd = sb("pwbd", [128, 2, 128], BF16)
            qkvT = sb("qkvT", [128, 12, TT], BF16)
            mixT = sb("mixT", [128, 8, TT], BF16)
            zs = sb("zs", [128, 4, TT], BF16)
            rawS = Slots("raw", [sb("raw%d" % i, [128, TT + 4], BF16) for i in range(3)])
            dgS = Slots("dg", [sb("dg%d" % i, [128, 128], BF16) for i in range(8)])
            accS = Slots("acc", [sb("acc%d" % i, [128, TT], F32) for i in range(2)])
            tmpS = Slots("tmpf", [sb("tmpf%d" % i, [128, TT], F32) for i in range(3)])
            sqS = Slots("sqb", [sb("sqb%d" % i, [128, TT], BF16) for i in range(2)])
            halo = sb("halo", [128, 12, 4], BF16)
            hpb = sb("hpb", [128, 2, TT + 15])
            s2b = sb("s2b", [128, TT + 15])
            s4b = sb("s4b", [128, TT + 15])
            pooledb = sb("pooledb", [128, TT], BF16)
            cbs = sb("cbs", [128, 2, TT], BF16)
            ccs = sb("ccs", [128, 2, TT], BF16)
            ub = sb("ub", [128, 2, TT + 2], BF16)
            colsA = sb("colsA", [128, 10, 16])
            Rl = sb("Rl", [128, 4, 2, 4])
            egl2 = sb("egl", [128, 2, 4, 2, 4])
            g4f = Slots("g4f", [sb("g4f%d" % i, [128, 4, 128], F32) for i in range(N_G4F)])
            g4b = Slots("g4b", [sb("g4b%d" % i, [128, 4, 128], BF16) for i in range(N_G4B)])
            PT = sb("PT", [128, 4, 4, 128], BF16)
            Kd = sb("Kd", [128, 4, 4, 128], BF16)
            WTb = sb("WTb", [128, 4, 4, 128], BF16)
            Ub = sb("Ub", [128, 4, 4, 128], BF16)
            qgT = sb("qgT", [128, 4, TT], BF16)
            Vn = sb("Vn", [128, 4, 128], BF16)
            S32 = sb("S32", [128, 4, 128])
            Sb = [sb("Sb%d" % i, [128, 4, 128], BF16) for i in range(2)]
            oT = sb("oT", [128, 4, TT])
            negA = sb("negA", [128, 16])
            n_mixer = len(_tiles) - n_persist

            R_halo = lambda ch: Ref(halo[:, ch, 0:3], ("halo", ch))
            for ch in range(12):
                mset(R_halo(ch), 0.0)
            R_hpb = lambda ch: Ref(hpb[:, ch, :], ("hpb", ch))
            for ch in range(2):
                sc.op("pool", lambda ch=ch: nc.gpsimd.memset(hpb[:, ch, 0:15], 0.0), [], [R_hpb(ch)])
            R_ub = lambda ch: Ref(ub[:, ch, :], ("ub", ch))
            for ch in range(2):
                sc.op("pool", lambda ch=ch: nc.gpsimd.memset(ub[:, ch, 0:2], 0.0), [], [R_ub(ch)])
            mset(Ref(S32[:, :, :], "S32"), 0.0)
            mset(Ref(Sb[0][:, :, :], ("Sb", 0)), 0.0)
            R_wab = Ref(wab[:, :, :], "wab")
            load_w(wab[:, :, :], "wab", w_in_d[l, :, 2048:2056])
            R_pw = Ref(pwbd[:, :, :], "pwbd")
            mset(R_pw, 0.0)
            for gi in range(4):
                ch, hf = gi // 2, gi % 2
                sc.dma("pool", pwbd[hf * 64:(hf + 1) * 64, ch, hf * 64:(hf + 1) * 64], pool_w_d[l, gi, :, :], [], [R_pw])
            R_negA = Ref(negA[:, :], "negA")
            act(R_negA, vcol(vb + 89, 16), AF.Exp)
            ts(R_negA, R_negA, -1.0, ALU.mult)

            def proj_chunk(wt, wkey, j, tt_):
                ps = bank()
                for c in range(8):
                    mm(ps, Ref(wt[:, c, j * 128:(j + 1) * 128], wkey), Ref(hT[:, c, :], ("hT", c)),
                       start=(c == 0), stop=(c == 7))
                return ps

            def silu_to(out, in_):
                t, k = tmpp.get()
                T = Ref(t[:, :], k)
                sigmoid_to(T, in_, T)
                tt(out, in_, T, ALU.mult)

            def make_tile(tt_):
                CA = lambda q_: Ref(colsA[:, q_, :], ("colsA", q_))
                cav = lambda q_: colsA[:, q_, :].rearrange("p (s h) -> p s h", h=4)
                egl = egl2[:, tt_ % 2, :, :, :]
                eglk = ("egl", tt_ % 2)
                R_egl = Ref(egl, eglk)

                def colsc(q_, m, h):
                    return sub(CA(q_), colsA[:, q_, m * 4 + h:m * 4 + h + 1])

                def g_norm1():
                    ti = yield from galloc(tmpS)
                    sqi = []
                    for _ in range(2):
                        _i = yield from galloc(sqS)
                        sqi.append(_i)
                    b = yield from galloc(pb)
                    ps = Ref(PS[b][:, :], ("ps", b))
                    for c in range(8):
                        SQ = Ref(sqS.tiles[sqi[c % 2]][:, :], ("sqb", sqi[c % 2]))
                        act(SQ, X(c, tt_), AF.Square)
                        mm(ps, R_onesb, SQ, start=(c == 0), stop=(c == 7), inc=True)
                        if c % 2 == 1:
                            yield
                    for i in sqi:
                        sqS.release(i)
                    RS = Ref(tmpS.tiles[ti][:, :], ("tmpf", ti))
                    act(RS, ps, AF.Ln, scale=1.0 / D, bias=R_eps)
                    pb.release(b)
                    act(RS, RS, AF.Exp, scale=-0.5)
                    yield
                    for c in range(8):
                        stt(Ref(hT[:, c, :], ("hT", c)), X(c, tt_), vcol(vb + c), RS, ALU.mult, ALU.mult)
                    tmpS.release(ti)

                F_ = lambda pool, i: Ref(pool.tiles[i][:, :], (pool.name, i))

                def proj_ps(wi, j):
                    b = yield from galloc(pb)
                    ps = Ref(PS[b][:, :], ("ps", b))
                    wt = wstS.tiles[wi]
                    for c in range(8):
                        mm(ps, Ref(wt[:, c, j * 128:(j + 1) * 128], ("wst", wi)), Ref(hT[:, c, :], ("hT", c)),
                           start=(c == 0), stop=(c == 7))
                    proj_cnt[wi] = proj_cnt.get(wi, 0) + 1
                    return b, ps

                def g_sigmoid(T, in_):
                    act(T, in_, AF.Exp, scale=-1.0)
                    yield
                    act(T, T, AF.Ln, bias=R_one)
                    yield
                    act(T, T, AF.Exp, scale=-1.0)
                    yield

                def g_qkv_chunk(wi, j, chn):
                    ri = yield from galloc(rawS)
                    b, ps = yield from proj_ps(wi, j)
                    rt = rawS.tiles[ri]
                    RAW = F_(rawS, ri)
                    cp(sub(RAW, rt[:, 0:3]), R_halo(chn), eng=RAW_ENG)
                    cp(sub(RAW, rt[:, 3:3 + TT]), ps, eng=RAW_ENG)
                    pb.release(b)
                    cp(R_halo(chn), sub(RAW, rt[:, TT:TT + 3]), eng=RAW_ENG)
                    yield
                    cw = vb + 16 + chn * 4
                    dis = []
                    for jj in range(4):
                        di = yield from galloc(dgS)
                        ts(F_(dgS, di), R_identb, vcol(cw + jj), ALU.mult, 0.0, ALU.add, eng="pool")
                        dis.append(di)
                    yield
                    ti = yield from galloc(tmpS)
                    T = F_(tmpS, ti)
                    if chn < 8:
                        ai = yield from galloc(accS)
                        ACC = F_(accS, ai)
                    bc = yield from galloc(pb)
                    psc = Ref(PS[bc][:, :], ("ps", bc))
                    for jj in range(4):
                        mm(psc, F_(dgS, dis[jj]), sub(RAW, rt[:, jj:jj + TT]), start=(jj == 0), stop=(jj == 3))
                    for di in dis:
                        dgS.release(di)
                    rawS.release(ri)
                    yield
                    yield from g_sigmoid(T, psc)
                    dst = Ref(qkvT[:, chn, :], ("qkvT", chn))
                    if chn >= 8:
                        tt(dst, psc, T, ALU.mult)
                        pb.release(bc)
                        tmpS.release(ti)
                        vdone[0] += 1
                        return
                    tt(ACC, psc, T, ALU.mult)
                    pb.release(bc)
                    yield
                    si = yield from galloc(sqS)
                    SQ = F_(sqS, si)
                    if SQ_ENG == "act":
                        act(SQ, ACC, AF.Square)
                    else:
                        tt(SQ, ACC, ACC, ALU.mult)
                    yield
                    b2 = yield from galloc(pb)
                    ps2 = Ref(PS[b2][:, :], ("ps", b2))
                    mm(ps2, R_onesb, SQ)
                    sqS.release(si)
                    yield
                    act(T, ps2, AF.Ln, scale=1.0, bias=R_eps)
                    pb.release(b2)
                    yield
                    act(T, T, AF.Exp, scale=-0.5)
                    yield
                    if chn < 4:
                        stt(dst, ACC, 128.0 ** -0.5, T, ALU.mult, ALU.mult)
                    else:
                        tt(dst, ACC, T, ALU.mult)
                    tmpS.release(ti)
                    accS.release(ai)

                def g_z_chunk(wi, j, h):
                    ti = yield from galloc(tmpS)
                    T = F_(tmpS, ti)
                    b, ps = yield from proj_ps(wi, j)
                    yield from g_sigmoid(T, ps)
                    tt(Ref(zs[:, h, :], ("zs", h)), ps, T, ALU.mult)
                    pb.release(b)
                    tmpS.release(ti)

                def g_pool_chunk(wi, ch):
                    b, ps = yield from proj_ps(wi, ch)
                    HP = R_hpb(ch)
                    cp(sub(HP, hpb[:, ch, 15:15 + TT]), ps, eng="act")
                    pb.release(b)
                    yield
                    n = TT + 15
                    R_s2 = Ref(s2b[:, :], "s2b")
                    R_s4 = Ref(s4b[:, :], "s4b")
                    tt(sub(R_s2, s2b[:, 1:n]), sub(HP, hpb[:, ch, 1:n]), sub(HP, hpb[:, ch, 0:n - 1]), ALU.add)
                    yield
                    tt(sub(R_s4, s4b[:, 3:n]), sub(R_s2, s2b[:, 3:n]), sub(R_s2, s2b[:, 1:n - 2]), ALU.add)
                    yield
                    if ch == 1:
                        tt(sub(R_s2, s2b[:, 7:n]), sub(R_s4, s4b[:, 7:n]), sub(R_s4, s4b[:, 3:n - 4]), ALU.add)
                        yield
                        tt(sub(R_s4, s4b[:, 15:n]), sub(R_s2, s2b[:, 15:n]), sub(R_s2, s2b[:, 7:n - 8]), ALU.add)
                        yield
                    ai = yield from galloc(accS)
                    at = accS.tiles[ai]
                    M = F_(accS, ai)
                    wins = (2, 4) if ch == 0 else (8, 16)
                    for hf in range(2):
                        src_t = s2b if hf == 0 else s4b
                        R_src = R_s2 if hf == 0 else R_s4
                        pp = slice(hf * 64, (hf + 1) * 64)
                        ts(sub(M, at[pp, :]), sub(R_src, src_t[pp, 15:15 + TT]), 1.0 / wins[hf], ALU.mult)
                        if tt_ == 0:
                            tt(sub(M, at[pp, 0:16]), sub(R_src, src_t[pp, 15:31]), sub(R_invc, invc[pp, ch, :]), ALU.mult)
                    yield
                    R_pooled = Ref(pooledb[:, :], "pooledb")
                    tt(R_pooled, M, sub(HP, hpb[:, ch, 15:15 + TT]), ALU.subtract)
                    accS.release(ai)
                    cp(sub(HP, hpb[:, ch, 0:15]), sub(HP, hpb[:, ch, TT:TT + 15]), eng="act")
                    yield
                    b2 = yield from galloc(pb)
                    ps2 = Ref(PS[b2][:, :], ("ps", b2))
                    mm(ps2, sub(R_pw, pwbd[:, ch, :]), R_pooled)
                    yield
                    act(Ref(mixT[:, 4 + ch, :], ("mixT", 4 + ch)), ps2, AF.Copy, scale=vcol(vb + 65 + ch))
                    pb.release(b2)

                def g_store_chunk(wi, ch, dst_t, nm):
                    b, ps = yield from proj_ps(wi, ch)
                    cp(Ref(dst_t[:, ch, :], (nm, ch)), ps, eng="act")
                    pb.release(b)
                    yield

                def g_sconv_chunk(wi, ch):
                    b, ps = yield from proj_ps(wi, ch)
                    U_ = R_ub(ch)
                    tt(sub(U_, ub[:, ch, 2:2 + TT]), Ref(ccs[:, ch, :], ("ccs", ch)), ps, ALU.mult)
                    pb.release(b)
                    yield
                    cw = vb + 67 + ch * 3
                    dis = []
                    for jj in range(3):
                        di = yield from galloc(dgS)
                        ts(F_(dgS, di), R_identb, vcol(cw + jj), ALU.mult, 0.0, ALU.add, eng="pool")
                        dis.append(di)
                    yield
                    bc = yield from galloc(pb)
                    psc = Ref(PS[bc][:, :], ("ps", bc))
                    for jj in range(3):
                        mm(psc, F_(dgS, dis[jj]), sub(U_, ub[:, ch, jj:jj + TT]), start=(jj == 0), stop=(jj == 2))
                    for di in dis:
                        dgS.release(di)
                    yield
                    cp(sub(U_, ub[:, ch, 0:2]), sub(U_, ub[:, ch, TT:TT + 2]), eng="act")
                    tt(Ref(mixT[:, 6 + ch, :], ("mixT", 6 + ch)), psc, Ref(cbs[:, ch, :], ("cbs", ch)), ALU.mult)
                    pb.release(bc)

                blocks = []
                qkv_blks = [(blk * 256, lambda wi, j, blk=blk: g_qkv_chunk(wi, j, blk * 2 + j)) for blk in range(6)]
                z_blks = [(1536 + blk * 256, lambda wi, j, blk=blk: g_z_chunk(wi, j, blk * 2 + j)) for blk in range(2)]
                if Z_IN_A:
                    blocks += qkv_blks[0:4] + z_blks + qkv_blks[4:6]
                else:
                    blocks += qkv_blks + z_blks
                blocks.append((2056, lambda wi, j: g_pool_chunk(wi, j)))
                blocks.append((2312, lambda wi, j: g_store_chunk(wi, j, cbs, "cbs")))
                blocks.append((2568, lambda wi, j: g_store_chunk(wi, j, ccs, "ccs")))
                blocks.append((2824, lambda wi, j: g_sconv_chunk(wi, j)))
                loaded = {}

                lim = [NA_BLOCKS - 1]
                vdone = [0]

                def issue_loads(upto):
                    upto = min(upto, lim[0])
                    for k in range(len(blocks)):
                        if k > upto:
                            break
                        if k not in loaded and wstS.free:
                            wi = wstS.alloc()
                            load_w(wstS.tiles[wi][:, :, :], ("wst", wi), w_in_d[l, :, blocks[k][0]:blocks[k][0] + 256])
                            loaded[k] = wi

                proj_cnt = {}

                def g_block(k):
                    issue_loads(k + 2)
                    while k not in loaded:
                        issue_loads(k)
                        if k not in loaded:
                            yield "blocked"
                    wi = loaded[k]
                    if PAIR_CHUNKS and k != 8:
                        proj_cnt[wi] = 0
                        released = False
                        for _ in g_inter([blocks[k][1](wi, 0), blocks[k][1](wi, 1)], 2):
                            if not released and proj_cnt[wi] >= 2:
                                wstS.release(wi)
                                released = True
                                issue_loads(k + 3)
                            yield
                        if not released:
                            wstS.release(wi)
                        return
                    g0 = blocks[k][1](wi, 0)
                    yield from g0
                    yield
                    g1 = blocks[k][1](wi, 1)
                    first = True
                    for _ in g1:
                        if first:
                            wstS.release(wi)
                            first = False
                            issue_loads(k + 3)
                        yield
                    if first:
                        wstS.release(wi)

                def g_cols():
                    bAB = yield from galloc(pa)
                    psAB = Ref(PS[bAB][:, 0:128], ("ps", bAB))
                    abv = psAB.ap[:, 0:32].rearrange("p (s e) -> p s e", e=8)
                    for s_ in range(4):
                        for c in range(8):
                            mm(sub(psAB, abv[:, s_, :]), Ref(hT[:, c, s_ * 128:(s_ + 1) * 128], ("hT", c)),
                               sub(R_wab, wab[:, c, :]), start=(c == 0), stop=(c == 7))
                    sc.op("dve", lambda: nc.vector.tensor_tensor(out=cav(5), in0=abv[:, :, 0:4],
                                                                 in1=vec[:, vb + 73:vb + 89].rearrange("p (s h) -> p s h", h=4),
                                                                 op=ALU.add), [psAB, VEC], [CA(5)])
                    act(CA(5), CA(5), AF.Exp)
                    act(CA(5), CA(5), AF.Ln, bias=R_one)
                    tt(CA(0), CA(5), R_negA, ALU.mult)
                    yield
                    sc.op("act", lambda: nc.scalar.activation(out=cav(6), in_=abv[:, :, 4:8], func=AF.Exp, scale=-1.0),
                          [psAB], [CA(6)])
                    act(CA(6), CA(6), AF.Ln, bias=R_one)
                    act(CA(1), CA(6), AF.Exp, scale=-1.0)
                    yield
                    pa.release(bAB)
                    bG = yield from galloc(pa)
                    psG = Ref(PS[bG][:, 0:128], ("ps", bG))
                    gcp = sub(psG, psG.ap[:, 0:16])
                    glp = sub(psG, psG.ap[:, 16:32])
                    eglp = sub(psG, psG.ap[:, 32:64])
                    mm(gcp, R_Mincl, CA(0))
                    mm(glp, R_Msame, CA(0))
                    for j in range(2):
                        sc.op("dve", lambda j=j: nc.vector.tensor_scalar(out=Rl[:, :, j, :], in0=cav(0), scalar1=ind[:, j:j + 1],
                                                                           scalar2=None, op0=ALU.mult),
                              [CA(0), R_ind], [Ref(Rl[:, :, :, :], "Rl")])
                    mm(eglp, R_onesf, Ref(Rl[:, :, :, :].rearrange("p s j h -> p (s j h)"), "Rl"))
                    ts(CA(2), gcp, -1.0, ALU.mult)
                    act(CA(7), gcp, AF.Exp)
                    tt(CA(3), CA(1), CA(7), ALU.mult)
                    yield
                    tt(CA(6), glp, CA(2), ALU.add)
                    act(CA(4), CA(6), AF.Exp)
                    act(Ref(egl2[:, tt_ % 2, :, :, :].rearrange("p s j h -> p (s j h)"), eglk), eglp, AF.Exp)
                    pa.release(bG)


                B4 = lambda t: t[:, :, :]
                MsT_b4 = Ref(MsT[:, :].unsqueeze(1).to_broadcast([128, 4, 128]), "MsT")
                idf_b4 = Ref(ident_f[:, :].unsqueeze(1).to_broadcast([128, 4, 128]), "ident_f")

                def psv(b):
                    return PS[b][:, :].rearrange("p (h t) -> p h t", h=4)

                def mm4(b, lfn, rfn):
                    pe_fill()
                    if BURST_EVERY:
                        _bc[0] += 1
                        if _bc[0] % BURST_EVERY == 0:
                            pe_fill(BURST_N)
                    P_ = Ref(psv(b), ("ps", b))
                    for h in range(4):
                        mm(sub(P_, PS[b][:, h * 128:(h + 1) * 128]), lfn(h), rfn(h))
                    return P_

                done = {}

                def gen_pre(m):
                    cols = slice(m * 128, (m + 1) * 128)
                    qT = lambda h: Ref(qkvT[:, h, cols], ("qkvT", h))
                    kT = lambda h: Ref(qkvT[:, 4 + h, cols], ("qkvT", 4 + h))
                    vT = lambda h: Ref(qkvT[:, 8 + h, cols], ("qkvT", 8 + h))
                    qT4 = Ref(qkvT[:, 0:4, cols], [("qkvT", h) for h in range(4)])
                    csb = lambda q_: Ref(colsA[:, q_, m * 4:(m + 1) * 4].unsqueeze(2).to_broadcast([128, 4, 128]),
                                         ("colsA", q_))
                    f4 = lambda i: Ref(g4f.tiles[i][:, :, :], ("g4f", i))
                    b4 = lambda i: Ref(g4b.tiles[i][:, :, :], ("g4b", i))
                    f4h = lambda i, h: Ref(g4f.tiles[i][:, h, :], ("g4f", i))
                    b4h = lambda i, h: Ref(g4b.tiles[i][:, h, :], ("g4b", i))
                    ig = yield from galloc(g4f)
                    cp(f4(ig), csb(0), eng=BC_ENG)
                    ib = yield from galloc(g4b)
                    cp(b4(ib), csb(1), eng=BC_ENG)
                    ie = yield from galloc(g4b)
                    cp(b4(ie), csb(7), eng=BC_ENG)
                    yield
                    b = yield from galloc(pa)
                    psG_ = Ref(psv(b), ("ps", b))
                    for h in range(4):
                        o_ = sub(psG_, PS[b][:, h * 128:(h + 1) * 128])
                        mm(o_, f4h(ig, h), R_Mincl, start=True, stop=False)
                        mm(o_, R_identb, R_NEG, start=False, stop=True)
                    g4f.release(ig)
                    iE2 = yield from galloc(g4f)
                    for h in range(4):
                        act(f4h(iE2, h), sub(psG_, PS[b][:, h * 128:(h + 1) * 128]), AF.Exp, bias=colsc(2, m, h))
                    pa.release(b)
                    yield
                    b = yield from galloc(pa)
                    psE_ = mm4(b, lambda h: b4h(ie, h), lambda h: R_identb)
                    g4b.release(ie)
                    tt(Ref(qgT[:, :, cols], ("qgT", m)), qT4, psE_, ALU.mult)
                    pa.release(b)
                    yield
                    iGT = yield from galloc(g4f)
                    tt(f4(iGT), f4(iE2), MsT_b4, ALU.mult)
                    b = yield from galloc(pa)
                    psB_ = mm4(b, lambda h: b4h(ib, h), lambda h: R_identb)
                    g4b.release(ib)
                    tt(f4(iGT), f4(iGT), psB_, ALU.mult)
                    pa.release(b)
                    yield
                    b = yield from galloc(pa)
                    psK_ = mm4(b, kT, kT)
                    iB = yield from galloc(g4b)
                    tt(b4(iB), psK_, f4(iGT), ALU.mult)
                    pa.release(b)
                    g4f.release(iGT)
                    iR = yield from galloc(g4b)
                    tt(b4(iR), idf_b4, b4(iB), ALU.subtract)
                    yield
                    b = yield from galloc(pa)
                    psQ_ = mm4(b, kT, qT)
                    tt(Ref(PT[:, :, m, :], ("PT", m)), psQ_, f4(iE2), ALU.mult)
                    pa.release(b)
                    g4f.release(iE2)
                    yield
                    b = yield from galloc(pa)
                    ps_ = mm4(b, lambda h: b4h(iB, h), lambda h: R_identb)
                    iA = yield from galloc(g4b)
                    cp(b4(iA), ps_, eng="act")
                    pa.release(b)
                    yield
                    b = yield from galloc(pa)
                    ps_ = mm4(b, lambda h: b4h(iA, h), lambda h: b4h(iB, h))
                    iBn = yield from galloc(g4b)
                    cp(b4(iBn), ps_, eng=B1_ENG)
                    pa.release(b)
                    yield
                    b = yield from galloc(pa)
                    ps_ = mm4(b, lambda h: b4h(iB, h), lambda h: b4h(iA, h))
                    iAn = yield from galloc(g4b)
                    cp(b4(iAn), ps_, eng="act")
                    pa.release(b)
                    g4b.release(iB)
                    g4b.release(iA)
                    iB, iA = iBn, iAn
                    yield
                    for j in range(1, 6):
                        b = yield from galloc(pa)
                        ps_ = mm4(b, lambda h: b4h(iA, h), lambda h: b4h(iR, h))
                        iRn = yield from galloc(g4b)
                        tt(b4(iRn), b4(iR), ps_, ALU.add)
                        pa.release(b)
                        g4b.release(iR)
                        iR = iRn
                        yield
                        if j < 5:
                            b = yield from galloc(pa)
                            ps_ = mm4(b, lambda h: b4h(iA, h), lambda h: b4h(iB, h))
                            iBn = yield from galloc(g4b)
                            cp(b4(iBn), ps_, eng="act")
                            pa.release(b)
                            yield
                            b = yield from galloc(pa)
                            ps_ = mm4(b, lambda h: b4h(iB, h), lambda h: b4h(iA, h))
                            iAn = yield from galloc(g4b)
                            cp(b4(iAn), ps_, eng="act")
                            pa.release(b)
                            g4b.release(iB)
                            g4b.release(iA)
                            iB, iA = iBn, iAn
                            yield
                    g4b.release(iB)
                    g4b.release(iA)
                    b = yield from galloc(pa)
                    ps_ = mm4(b, kT, lambda h: R_identb)
                    iKT = yield from galloc(g4b)
                    tt(b4(iKT), ps_, csb(3), ALU.mult)
                    tt(Ref(Kd[:, :, m, :], ("Kd", m)), ps_, csb(4), ALU.mult)
                    pa.release(b)
                    yield
                    while vdone[0] < 4:
                        yield "blocked"
                    b = yield from galloc(pa)
                    ps_ = mm4(b, vT, lambda h: R_identb)
                    iVT = yield from galloc(g4b)
                    tt(b4(iVT), ps_, csb(1), ALU.mult)
                    pa.release(b)
                    yield
                    b = yield from galloc(pa)
                    ps_ = mm4(b, lambda h: b4h(iKT, h), lambda h: b4h(iR, h))
                    cp(Ref(WTb[:, :, m, :], ("WTb", m)), ps_, eng="act")
                    pa.release(b)
                    g4b.release(iKT)
                    yield
                    b = yield from galloc(pa)
                    ps_ = mm4(b, lambda h: b4h(iR, h), lambda h: b4h(iVT, h))
                    cp(Ref(Ub[:, :, m, :], ("Ub", m)), ps_, eng=B1_ENG)
                    pa.release(b)
                    g4b.release(iVT)
                    g4b.release(iR)
                    done[m] = True

                def g_scan():
                    R_S32a = Ref(S32[:, :, :], "S32")
                    R_Sba = lambda i: Ref(Sb[i][:, :, :], ("Sb", i))
                    for n in range(8):
                        m, j = n // 2, n % 2
                        while m not in done:
                            yield "blocked"
                        pp = slice(j * 64, (j + 1) * 64)
                        gch = tt_ * 8 + n
                        cur, nxt = gch % 2, (gch + 1) % 2
                        egb = Ref(egl2[:, tt_ % 2, m, j, :].unsqueeze(2).to_broadcast([128, 4, 128]), eglk)
                        R_Vn = Ref(Vn[pp, :, :], "Vn")
                        b1 = yield from galloc(pa)
                        psS = mm4(b1, lambda h: Ref(WTb[:, h, m, :], ("WTb", m)), lambda h: Ref(Sb[cur][:, h, :], ("Sb", cur)))
                        tt(R_Vn, Ref(Ub[pp, :, m, :], ("Ub", m)), sub(psS, psv(b1)[pp, :, :]), ALU.subtract)
                        pa.release(b1)
                        tt(R_S32a, R_S32a, egb, ALU.mult)
                        yield
                        b2 = yield from galloc(pa)
                        psO = Ref(PS[b2][:, 0:256].rearrange("p (h t) -> p h t", h=4), ("ps", b2))
                        for h in range(4):
                            o_ = sub(psO, PS[b2][:, h * 64:(h + 1) * 64])
                            mm(o_, Ref(Sb[cur][:, h, :], ("Sb", cur)), Ref(qgT[:, h, n * 64:(n + 1) * 64], ("qgT", m)),
                               start=True, stop=False)
                            mm(o_, Ref(Vn[pp, h, :], "Vn"), Ref(PT[pp, h, m, j * 64:(j + 1) * 64], ("PT", m)),
                               start=False, stop=True)
                        cp(Ref(oT[:, :, n * 64:(n + 1) * 64], ("oT", n)), psO, eng="act")
                        pa.release(b2)
                        yield
                        b3 = yield from galloc(pa)
                        psU = mm4(b3, lambda h: Ref(Kd[pp, h, m, :], ("Kd", m)), lambda h: Ref(Vn[pp, h, :], "Vn"))
                        tt(R_Sba(nxt), R_S32a, psU, ALU.add)
                        tt(R_S32a, R_S32a, psU, ALU.add)
                        pa.release(b3)
                        yield

                def g_onorm(h):
                    R_o = Ref(oT[:, h, :], [("oT", n) for n in range(8)])
                    ti = yield from galloc(tmpS)
                    si = yield from galloc(sqS)
                    SQ = Ref(sqS.tiles[si][:, :], ("sqb", si))
                    act(SQ, R_o, AF.Square)
                    yield
                    b = yield from galloc(pb)
                    ps2 = Ref(PS[b][:, :], ("ps", b))
                    mm(ps2, R_onesb, SQ)
                    sqS.release(si)
                    yield
                    RS = Ref(tmpS.tiles[ti][:, :], ("tmpf", ti))
                    act(RS, ps2, AF.Ln, scale=1.0 / 128, bias=R_eps)
                    pb.release(b)
                    yield
                    act(RS, RS, AF.Exp, scale=-0.5)
                    yield
                    stt(RS, R_o, vcol(vb + 64), RS, ALU.mult, ALU.mult)
                    yield
                    tt(Ref(mixT[:, h, :], ("mixT", h)), RS, Ref(zs[:, h, :], ("zs", h)), ALU.mult)
                    tmpS.release(ti)

                def dbg_dumps():
                    pass
                    if dbg is not None and dbg == ("qkv", l):
                        for c in range(8):
                            sc.dma("pool", dbg_d[c * 128:(c + 1) * 128, tt_ * TT:(tt_ + 1) * TT], qkvT[:, c, :],
                                   [Ref(qkvT[:, c, :], ("qkvT", c))], [])
                    if dbg is not None and dbg == ("o", l):
                        for c in range(4):
                            sc.dma("sp", dbg_d[c * 128:(c + 1) * 128, tt_ * TT:(tt_ + 1) * TT], oT[:, c, :],
                                   [Ref(oT[:, c, :], [("oT", n) for n in range(8)])], [])
                    if dbg is not None and dbg == ("mixed", l):
                        for c in range(8):
                            sc.dma("pool", dbg_d[c * 128:(c + 1) * 128, tt_ * TT:(tt_ + 1) * TT], mixT[:, c, :],
                                   [Ref(mixT[:, c, :], ("mixT", c))], [])
                def g_wout():
                    for blk in range(4):
                        wi = yield from galloc(wstS)
                        wt = wstS.tiles[wi]
                        load_w(wt[:, :, :], ("wst", wi), w_out_d[l, :, blk * 256:(blk + 1) * 256])
                        bs = []
                        for j in range(2):
                            b = yield from galloc(pb)
                            bs.append(b)
                        for j in range(2):
                            ps = Ref(PS[bs[j]][:, :], ("ps", bs[j]))
                            for c in range(8):
                                mm(ps, Ref(wt[:, c, j * 128:(j + 1) * 128], ("wst", wi)), Ref(mixT[:, c, :], ("mixT", c)),
                                   start=(c == 0), stop=(c == 7))
                        wstS.release(wi)
                        yield
                        for j in range(2):
                            o_ = blk * 2 + j
                            ps = Ref(PS[bs[j]][:, :], ("ps", bs[j]))
                            tt(X(o_, tt_), X(o_, tt_), ps, ALU.add)
                            pb.release(bs[j])
                        yield

                def gA():
                    yield from g_norm1()
                    if COLS_IN_A:
                        yield from g_inter([g_cols()] + [g_block(k) for k in range(NA_BLOCKS)], CHUNK_WIDTH + 1)
                    else:
                        yield from g_inter([g_block(k) for k in range(NA_BLOCKS)], CHUNK_WIDTH)

                def g_gdn_pre():
                    if not COLS_IN_A:
                        yield from g_cols()
                    yield from g_inter([gen_pre(m) for m in range(4)], PRE_WIDTH)

                def gB():
                    lim[0] = len(blocks) - 1
                    gl = [speedup(g_gdn_pre(), PRE_SPEED)]
                    if SCAN_IN_B:
                        gl.append(g_scan())
                    yield from g_inter(gl + [g_block(k) for k in range(NA_BLOCKS, len(blocks))], len(gl) + B_EXTRA)

                def gC():
                    if not SCAN_IN_B:
                        yield from g_scan()
                    yield from g_inter([g_onorm(h) for h in range(4)], 2)
                    dbg_dumps()
                    yield from g_wout()

                return gA, gB, gC

            tiles_ = [make_tile(t) for t in range(NT)]
            run_interleaved([tiles_[0][0]()], 1)
            run_interleaved([tiles_[0][1]()], 1)
            for t in range(NT):
                if t + 1 < NT:
                    run_interleaved([speedup(tiles_[t][2](), C_SPEED), speedup(tiles_[t + 1][0](), A_SPEED)], 2)
                    run_interleaved([tiles_[t + 1][1]()], 1)
                else:
                    run_interleaved([tiles_[t][2]()], 1)

            sc.barrier()
            free_tiles(n_mixer)
            checkpoint("x1", l)
            checkpoint("mixed", l)
            checkpoint("qkv", l)
            checkpoint("o", l)

            h2T = sb("h2T", [128, 8, S], BF16)
            ffp = Pool_("ffT", [128, 2, S], BF16, 2)
            wgp = Pool_("wg", [128, 8, 256], BF16, 2)
            wup = Pool_("wu", [128, 8, 256], BF16, 2)
            wdp = Pool_("wd", [128, 2, D], BF16, 2)
            tmpp = Pool_("tmpf", [128, TT], F32, 4)
            sqp = Pool_("sqb", [128, TT], BF16, 2)
            pTb = sb("pTb", [128, 2, S], BF16)
            ppw = sb("ppw", [128, 2, D], BF16)
            pgp = Pool_("pg", [128, 8, 256], BF16, 2)
            n_ffn = len(_tiles) - n_persist
            H2 = lambda c, t: Ref(h2T[:, c, t * TT:(t + 1) * TT], ("h2T", c, t))

            for c2 in range(2):
                for hf in range(2):
                    sc.dma("pool", pTb[:, c2, hf * 1024:(hf + 1) * 1024],
                           pT_d[l, c2 * 128:(c2 + 1) * 128, hf * 1024:(hf + 1) * 1024], [], [Ref(pTb[:, c2, :], ("pTb", c2, hf))])
            load_w(ppw[:, :, :], "ppw", ple_proj_d[l, :, :])

            for tt_ in range(NT):
                ps = bank()
                for c in range(8):
                    t, k = sqp.get()
                    SQ = Ref(t[:, :], k)
                    act(SQ, X(c, tt_), AF.Square)
                    mm(ps, R_onesb, SQ, start=(c == 0), stop=(c == 7), inc=True)
                t, k = tmpp.get()
                LN = Ref(t[:, :], k)
                t, k = tmpp.get()
                RS = Ref(t[:, :], k)
                rstd_from_sumsq(ps, RS, 1.0 / D, LN)
                for c in range(8):
                    stt(H2(c, tt_), X(c, tt_), vcol(vb + 8 + c), RS, ALU.mult, ALU.mult)

            for grp in range(NF // 2):
                f0 = grp * 256
                wg, wgk = wgp.get()
                load_w(wg[:, :, :], wgk, w_gate_d[l, :, f0:f0 + 256])
                wu, wuk = wup.get()
                load_w(wu[:, :, :], wuk, w_up_d[l, :, f0:f0 + 256])
                wd, wdk = wdp.get()
                load_w(wd[:, :, :], wdk, w_down_d[l, f0:f0 + 256, :])
                ff, ffk = ffp.get()
                FF = lambda fi, t: Ref(ff[:, fi, t * TT:(t + 1) * TT], (ffk, fi, t))
                for fi in range(2):
                    for tt_ in range(NT):
                        psg = bank()
                        for c in range(8):
                            mm(psg, Ref(wg[:, c, fi * 128:(fi + 1) * 128], wgk), H2(c, tt_), start=(c == 0), stop=(c == 7))
                        psu = bank()
                        for c in range(8):
                            mm(psu, Ref(wu[:, c, fi * 128:(fi + 1) * 128], wuk), H2(c, tt_), start=(c == 0), stop=(c == 7))
                        t, k = tmpp.get()
                        SG = Ref(t[:, :], k)
                        sigmoid_to(SG, psg, SG)
                        tt(SG, SG, psg, ALU.mult)
                        tt(FF(fi, tt_), SG, psu, ALU.mult)
                for o_ in range(8):
                    for tt_ in range(NT):
                        ps = bank()
                        for fi in range(2):
                            mm(ps, Ref(wd[:, fi, o_ * 128:(o_ + 1) * 128], wdk), FF(fi, tt_), start=(fi == 0), stop=(fi == 1))
                        tt(X(o_, tt_), X(o_, tt_), ps, ALU.add)

            checkpoint("x2", l)
            for c in range(8):
                for tt_ in range(NT):
                    cp(H2(c, tt_), X(c, tt_), eng=("act" if (c + tt_) % 2 else "dve"))
            for blk in range(4):
                pg, pgk = pgp.get()
                load_w(pg[:, :, :], pgk, ple_gate_d[l, :, blk * 256:(blk + 1) * 256])
                for j in range(2):
                    o_ = blk * 2 + j
                    for tt_ in range(NT):
                        psg = bank()
                        for c in range(8):
                            mm(psg, Ref(pg[:, c, j * 128:(j + 1) * 128], pgk), H2(c, tt_), start=(c == 0), stop=(c == 7))
                        psp = bank()
                        for c2 in range(2):
                            mm(psp, Ref(ppw[:, c2, o_ * 128:(o_ + 1) * 128], "ppw"),
                               Ref(pTb[:, c2, tt_ * TT:(tt_ + 1) * TT], ("pTb", c2, tt_ // 2)), start=(c2 == 0), stop=(c2 == 1))
                        t, k = tmpp.get()
                        SG = Ref(t[:, :], k)
                        sigmoid_to(SG, psg, SG)
                        tt(SG, SG, psp, ALU.mult)
                        tt(X(o_, tt_), X(o_, tt_), SG, ALU.add)
            sc.barrier()
            free_tiles(n_ffn)


    def _final():
        if 'final' in _skip:
            for c in range(8):
                for t in range(NT):
                    sc.dma("sp", yT_d[c * 128:(c + 1) * 128, t * TT:(t + 1) * TT], xT[:, c, t * TT:(t + 1) * TT], [X(c, t)], [])
            return
        tmpp = Pool_("tmpf", [128, TT], F32, 4)
        sqp = Pool_("sqb", [128, TT], BF16, 2)
        outp = Pool_("outb", [128, TT], F32, 4)
        if dbg is not None and dbg[0] not in ("mixed", "qkv", "o"):
            for c in range(8):
                for t in range(NT):
                    sc.dma("sp", dbg_d[c * 128:(c + 1) * 128, t * TT:(t + 1) * TT], xT[:, c, t * TT:(t + 1) * TT], [X(c, t)], [])
        for tt_ in range(NT):
            ps = bank()
            for c in range(8):
                t, k = sqp.get()
                SQ = Ref(t[:, :], k)
                act(SQ, X(c, tt_), AF.Square)
                mm(ps, R_onesb, SQ, start=(c == 0), stop=(c == 7), inc=True)
            t, k = tmpp.get()
            LN = Ref(t[:, :], k)
            t, k = tmpp.get()
            RS = Ref(t[:, :], k)
            rstd_from_sumsq(ps, RS, 1.0 / D, LN)
            for c in range(8):
                t, k = outp.get()
                OB = Ref(t[:, :], k)
                stt(OB, X(c, tt_), vcol(2 * VL + c), RS, ALU.mult, ALU.mult)
                sc.dma("sp", yT_d[c * 128:(c + 1) * 128, tt_ * TT:(tt_ + 1) * TT], t[:, :], [OB], [])
    try:
        _layers(n_layers)
    except StopBuild:
        sc.barrier()
        free_tiles(len(_tiles) - n_persist)
    _final()
    sc.finish()
    return nc


def pack_vec(inp):
    vec = np.zeros((128, NV), np.float32)
    for l in range(2):
        b = l * VL
        vec[:, b:b + 8] = inp["norm1_g"][l].reshape(8, 128).T
        vec[:, b + 8:b + 16] = inp["norm2_g"][l].reshape(8, 128).T
        cq = inp["conv_qkv"][l]
        vec[:, b + 16:b + 64] = cq.reshape(4, 12, 128).transpose(2, 1, 0).reshape(128, 48)
        vec[:, b + 64] = inp["onorm_g"][l]
        vec[:, b + 65:b + 67] = inp["pool_scale"][l].reshape(2, 128).T
        sw = inp["sconv_w"][l]
        vec[:, b + 67:b + 73] = sw.reshape(3, 2, 128).transpose(2, 1, 0).reshape(128, 6)
        vec[:, b + 73:b + 89] = np.broadcast_to(np.tile(inp["dt_bias"][l], 4)[None, :], (128, 16))
        vec[:, b + 89:b + 105] = np.broadcast_to(np.tile(inp["a_log"][l], 4)[None, :], (128, 16))
    vec[:, 2 * VL:2 * VL + 8] = inp["final_g"].reshape(8, 128).T
    return vec


_NC_CACHE = {}


def kernel(**inputs):
    inp = {k: np.asarray(v) for k, v in inputs.items()}
    x = inp["x"].astype(np.float32, copy=False)
    p = inp["p"].astype(np.float32, copy=False)
    vec = pack_vec(inp)
    shared = {
        "w_in": np.ascontiguousarray(inp["w_in"], np.float32),
        "w_out": np.ascontiguousarray(inp["w_out"], np.float32),
        "w_gate": np.ascontiguousarray(inp["w_gate"], np.float32),
        "w_up": np.ascontiguousarray(inp["w_up"], np.float32),
        "w_down": np.ascontiguousarray(inp["w_down"], np.float32),
        "ple_proj": np.ascontiguousarray(inp["ple_proj"], np.float32),
        "ple_gate": np.ascontiguousarray(inp["ple_gate"], np.float32),
        "pool_w": np.ascontiguousarray(inp["pool_w"], np.float32),
        "vec": vec,
    }
    in_maps = []
    for b in range(N_CORES):
        m = dict(shared)
        m["xT"] = np.ascontiguousarray(x[b].T)
        m["pT"] = np.ascontiguousarray(p[:, b].transpose(0, 2, 1))
        in_maps.append(m)
    if "nc" not in _NC_CACHE:
        _NC_CACHE["nc"] = build()
    res = run_bass_kernel_spmd(_NC_CACHE["nc"], in_maps, core_ids=list(range(N_CORES)))
    out = np.stack([np.ascontiguousarray(r["yT"].T) for r in res.results], axis=0)
    return out.astype(np.float32, copy=False)
```

```python
import os
import numpy as np
import concourse.bass as bass
import concourse.mybir as mybir
from concourse.bass_utils import run_bass_kernel_spmd

F32 = mybir.dt.float32
BF16 = mybir.dt.bfloat16
AF = mybir.ActivationFunctionType
ALU = mybir.AluOpType

D = 1024
S = 2048
TT = 512
NT = S // TT
DIN = 3080
DFF = 2816
NF = DFF // 128
EPS = 1e-6
VL = 105
NV = 2 * VL + 8
N_CORES = 8
PRE_WIDTH = int(os.environ.get('K_PRE_WIDTH', '2'))
N_FILL = int(os.environ.get('K_FILL', '0'))
BURST_EVERY = int(os.environ.get('K_BURST_EVERY', '0'))
BURST_N = int(os.environ.get('K_BURST_N', '16'))
COLS_IN_A = bool(int(os.environ.get('K_COLS_IN_A', '1')))
SCAN_IN_B = bool(int(os.environ.get('K_SCAN_IN_B', '0')))
CHUNK_WIDTH = int(os.environ.get('K_CHUNK_WIDTH', '2'))
RAW_ENG = os.environ.get('K_RAW_ENG', 'dve')
SQ_ENG = os.environ.get('K_SQ_ENG', 'act')
B1_ENG = os.environ.get('K_B1_ENG', 'act')
BC_ENG = os.environ.get('K_BC_ENG', 'dve')
C_SPEED = int(os.environ.get('K_C_SPEED', '1'))
A_SPEED = int(os.environ.get('K_A_SPEED', '2'))
NA_BLOCKS = int(os.environ.get('K_NA_BLOCKS', '4'))
PAIR_CHUNKS = bool(int(os.environ.get('K_PAIR', '1')))
N_G4F = 6
N_G4B = 15
SAME_ENGINE_NOWAIT = bool(int(os.environ.get('K_SE_NOWAIT', '0')))


class Ref:
    __slots__ = ("ap", "keys")

    def __init__(self, ap, keys):
        self.ap = ap
        self.keys = keys if isinstance(keys, list) else [keys]


class Sched:
    def __init__(self, nc, n_dma_sems=8):
        self.nc = nc
        self.E = {"pe": nc.tensor, "act": nc.scalar, "dve": nc.vector, "pool": nc.gpsimd, "sp": nc.sync}
        self.sem = {}
        self.cnt = {}
        for e in self.E:
            self.sem[e] = nc.semaphore("s_" + e).__enter__()
            self.cnt[e] = 0
        self.seen = {e: {} for e in self.E}
        self.lastw = {}
        self.readers = {}
        self.dsem = {}
        self.dtot = {}
        self.drr = {}
        for q in ("sp", "pool"):
            self.dsem[q] = [nc.semaphore("d_%s%d" % (q, i)).__enter__() for i in range(n_dma_sems)]
            self.dtot[q] = [0] * n_dma_sems
            self.drr[q] = 0

    def _wait(self, e, dep):
        sem, val, src = dep
        if src == e and (e == "pe" or (SAME_ENGINE_NOWAIT and e in ("act", "dve"))):
            return
        k = id(sem)
        if self.seen[e].get(k, 0) >= val:
            return
        self.E[e].wait_ge(sem, val)
        self.seen[e][k] = val

    def _deps(self, reads, writes):
        deps = []
        for r in reads:
            for k in r.keys:
                if k in self.lastw:
                    deps.append(self.lastw[k])
        for w in writes:
            for k in w.keys:
                if k in self.lastw:
                    deps.append(self.lastw[k])
                rd = self.readers.get(k)
                if rd:
                    deps.extend(rd.values())
        return deps

    def _record(self, tag, reads, writes):
        for w in writes:
            for k in w.keys:
                self.lastw[k] = tag
                self.readers[k] = {}
        for r in reads:
            for k in r.keys:
                self.readers.setdefault(k, {})[id(tag[0])] = tag

    def op(self, e, fn, reads, writes, inc=True):
        for d in self._deps(reads, writes):
            self._wait(e, d)
        ins = fn()
        if inc:
            ins.then_inc(self.sem[e], 1)
            self.cnt[e] += 1
            tag = (self.sem[e], self.cnt[e], e)
        else:
            tag = (self.sem[e], self.cnt[e] + 1, e)
        self._record(tag, reads, writes)
        return ins

    def dma(self, q, out, in_, reads, writes, **kw):
        i = self.drr[q]
        self.drr[q] = (i + 1) % len(self.dsem[q])
        sem = self.dsem[q][i]
        tot = self.dtot[q][i]
        for d in self._deps(reads, writes):
            self._wait(q, d)
        if tot > 0:
            self._wait(q, (sem, tot, "dma"))
        ins = self.E[q].dma_start(out=out, in_=in_, **kw)
        ins.then_inc(sem, 16)
        tot += 16
        self.dtot[q][i] = tot
        self._record((sem, tot, "dma" + q), reads, writes)

    def barrier(self):
        for e in self.E:
            for f in self.E:
                if f != e and self.cnt[f] > 0:
                    self._wait(e, (self.sem[f], self.cnt[f], f))
            for q in ("sp", "pool"):
                for sem, tot in zip(self.dsem[q], self.dtot[q]):
                    if tot > 0:
                        self._wait(e, (sem, tot, "dma"))

    def finish(self):
        for q in ("sp", "pool"):
            for sem, tot in zip(self.dsem[q], self.dtot[q]):
                if tot > 0:
                    self._wait(q, (sem, tot, "dma"))


class StopBuild(Exception):
    pass


def build(n_layers=2, dbg=None):
    nc = bass.Bass("TRN2", target_bir_lowering=False)
    sc = Sched(nc)
    dr = lambda name, shape, kind="ExternalInput": nc.dram_tensor(name, shape, F32, kind=kind).ap()
    xT_d = dr("xT", [D, S])
    pT_d = dr("pT", [2, 256, S])
    w_in_d = dr("w_in", [2, D, DIN])
    w_out_d = dr("w_out", [2, D, D])
    w_gate_d = dr("w_gate", [2, D, DFF])
    w_up_d = dr("w_up", [2, D, DFF])
    w_down_d = dr("w_down", [2, DFF, D])
    ple_proj_d = dr("ple_proj", [2, 256, D])
    ple_gate_d = dr("ple_gate", [2, D, D])
    pool_w_d = dr("pool_w", [2, 4, 64, 64])
    vec_d = dr("vec", [128, NV])
    yT_d = dr("yT", [D, S], kind="ExternalOutput")
    dbg_d = None
    if dbg is not None:
        dbg_d = dr("dbg", [D, S], kind="ExternalOutput")

    _tiles = []
    _uid = [0]

    def sb(name, shape, dt=F32):
        _uid[0] += 1
        g = nc.sbuf_tensor("t%d_%s" % (_uid[0], name), shape, dt)
        t = g.__enter__()
        _tiles.append(g)
        return t

    def free_tiles(n):
        for _ in range(n):
            _tiles.pop().__exit__(None, None, None)

    PS = [nc.psum_tensor("ps%d" % i, [128, 512], F32).__enter__() for i in range(8)]

    def mm(out, lhsT, rhs, start=True, stop=True, inc=None, extra_reads=()):
        if inc is None:
            inc = stop
        return sc.op("pe", lambda: nc.tensor.matmul(out.ap, lhsT=lhsT.ap, rhs=rhs.ap, start=start, stop=stop),
                     [lhsT, rhs] + list(extra_reads), [out], inc=inc)

    def act(out, in_, func, scale=None, bias=None, extra=()):
        kw = {}
        rd = [in_] + list(extra)
        if scale is not None:
            if isinstance(scale, Ref):
                kw["scale"] = scale.ap
                rd.append(scale)
            else:
                kw["scale"] = float(scale)
        if bias is not None:
            if isinstance(bias, Ref):
                kw["bias"] = bias.ap
                rd.append(bias)
            else:
                kw["bias"] = float(bias)
        return sc.op("act", lambda: nc.scalar.activation(out=out.ap, in_=in_.ap, func=func, **kw), rd, [out])

    def tt(out, a, b, op, eng="dve"):
        e = nc.vector if eng == "dve" else nc.gpsimd
        return sc.op(eng, lambda: e.tensor_tensor(out=out.ap, in0=a.ap, in1=b.ap, op=op), [a, b], [out])

    def ts(out, a, s1, op0, s2=None, op1=None, eng="dve"):
        e = nc.vector if eng == "dve" else nc.gpsimd
        rd = [a]
        v1 = s1
        if isinstance(s1, Ref):
            rd.append(s1)
            v1 = s1.ap
        v2 = s2
        if isinstance(s2, Ref):
            rd.append(s2)
            v2 = s2.ap
        if op1 is None:
            return sc.op(eng, lambda: e.tensor_scalar(out=out.ap, in0=a.ap, scalar1=v1, scalar2=None, op0=op0), rd, [out])
        return sc.op(eng, lambda: e.tensor_scalar(out=out.ap, in0=a.ap, scalar1=v1, scalar2=v2, op0=op0, op1=op1),
                     rd, [out])

    def stt(out, a, s, b, op0, op1):
        rd = [a, b]
        v = s
        if isinstance(s, Ref):
            rd.append(s)
            v = s.ap
        return sc.op("dve", lambda: nc.vector.scalar_tensor_tensor(out=out.ap, in0=a.ap, scalar=v, in1=b.ap,
                                                                   op0=op0, op1=op1), rd, [out])

    def cp(out, in_, eng="dve"):
        if eng == "act":
            return act(out, in_, AF.Copy)
        e = nc.vector if eng == "dve" else nc.gpsimd
        return sc.op(eng, lambda: e.tensor_copy(out=out.ap, in_=in_.ap), [in_], [out])

    def mset(t, val, eng="pool"):
        e = nc.vector if eng == "dve" else nc.gpsimd
        return sc.op(eng, lambda: e.memset(t.ap, val), [], [t])

    xT = sb("xT", [128, 8, S])
    vec = sb("vec", [128, NV])
    ident_f = sb("ident_f", [128, 128])
    ident_b = sb("ident_b", [128, 128], BF16)
    ones_b = sb("ones_b", [128, 128], BF16)
    ones_f = sb("ones_f", [128, 128])
    MsT = sb("MsT", [128, 128])
    Mincl = sb("Mincl", [128, 128])
    Msame = sb("Msame", [128, 128])
    ind = sb("ind", [128, 2])
    invc = sb("invc", [128, 2, 16])

    X = lambda c, t: Ref(xT[:, c, t * TT:(t + 1) * TT], ("xT", c, t))
    VEC = Ref(vec[:, :], "vec")

    def vcol(j, n=1):
        return Ref(vec[:, j:j + n], "vec")

    R_identf = Ref(ident_f[:, :], "ident_f")
    R_identb = Ref(ident_b[:, :], "ident_b")
    R_onesb = Ref(ones_b[:, :], "ones_b")
    R_onesf = Ref(ones_f[:, :], "ones_f")
    R_MsT = Ref(MsT[:, :], "MsT")
    R_Mincl = Ref(Mincl[:, :], "Mincl")
    R_Msame = Ref(Msame[:, :], "Msame")

    sc.dma("sp", vec[:, :], vec_d[:, :], [], [VEC])
    for t in range(NT):
        for c in range(8):
            sc.dma("sp", xT[:, c, t * TT:(t + 1) * TT], xT_d[c * 128:(c + 1) * 128, t * TT:(t + 1) * TT], [], [X(c, t)])

    _skip = os.environ.get('KSKIP', '').split(',')
    mset(R_onesf, 1.0)
    mset(R_onesb, 1.0)
    sc.op("pool", lambda: nc.gpsimd.affine_select(out=ident_f[:, :], in_=ones_f[:, :], pattern=[[1, 128]],
                                                  compare_op=ALU.is_equal, fill=0.0, base=0, channel_multiplier=-1),
          [R_onesf], [R_identf])
    sc.op("pool", lambda: nc.gpsimd.affine_select(out=MsT[:, :], in_=ones_f[:, :], pattern=[[1, 128]],
                                                  compare_op=ALU.is_gt, fill=0.0, base=0, channel_multiplier=-1),
          [R_onesf], [R_MsT])
    if 'msub' not in _skip:
        sc.op("pool", lambda: nc.gpsimd.memset(MsT[0:64, 64:128], 0.0), [], [R_MsT])
    cp(R_identb, R_identf, eng="pool")
    tt(R_Mincl, R_MsT, R_identf, ALU.add, eng="pool")
    NEGb = sb("NEGb", [128, 128], BF16)
    R_NEG = Ref(NEGb[:, :], "NEGb")
    ts(R_NEG, R_Mincl, -1.0, ALU.add, 30000.0, ALU.mult)
    mset(R_Msame, 0.0)
    if 'msub' not in _skip:
        sc.op("pool", lambda: nc.gpsimd.memset(Msame[0:64, 0:64], 1.0), [], [R_Msame])
        sc.op("pool", lambda: nc.gpsimd.memset(Msame[64:128, 64:128], 1.0), [], [R_Msame])
    R_ind = Ref(ind[:, :], "ind")
    mset(R_ind, 0.0)
    if 'msub' not in _skip:
        sc.op("pool", lambda: nc.gpsimd.memset(ind[0:64, 0:1], 1.0), [], [R_ind])
        sc.op("pool", lambda: nc.gpsimd.memset(ind[64:128, 1:2], 1.0), [], [R_ind])
    R_invc = Ref(invc[:, :, :], "invc")
    for ch in range(2):
        sc.op("pool", lambda ch=ch: nc.gpsimd.iota(invc[:, ch, :], pattern=[[1, 16]], base=1, channel_multiplier=0,
                                                  allow_small_or_imprecise_dtypes=True), [], [R_invc])
    for ch in range(2):
        for hf in range(2):
            if 'invc' in _skip:
                continue
            win = (2, 4, 8, 16)[ch * 2 + hf]
            sl = invc[hf * 64:(hf + 1) * 64, ch, :]
            sc.op("dve", lambda sl=sl, win=win: nc.vector.tensor_scalar(out=sl, in0=sl, scalar1=float(win), scalar2=None,
                                                                          op0=ALU.min), [R_invc], [R_invc])
    sc.op("dve", lambda: nc.vector.reciprocal(out=invc[:, :, :], in_=invc[:, :, :]), [R_invc], [R_invc])

    bank_rr = [0]

    def bank():
        i = bank_rr[0]
        bank_rr[0] = (i + 1) % 8
        return Ref(PS[i][:, :], ("ps", i))

    _all_slots = []

    class Slots:
        def __init__(self, name, tiles, ids=None):
            _all_slots.append(self)
            self.name = name
            self.tiles = tiles
            self.free = list(ids if ids is not None else range(len(tiles)))

        def alloc(self):
            if not self.free:
                raise RuntimeError("pool exhausted: " + self.name)
            return self.free.pop(0)

        def release(self, i):
            self.free.append(i)

    pa = Slots("ps", PS, ids=[4, 5, 6, 7] if (N_FILL or BURST_EVERY) else [3, 4, 5, 6, 7])
    pb = Slots("ps", PS, ids=[0, 1, 2])

    def galloc(pool):
        while not pool.free:
            yield "blocked"
        return pool.alloc()

    def _progress():
        return sum(sc.cnt.values()) + sum(sum(v) for v in sc.dtot.values())

    def g_inter(gens, width):
        pending = list(gens)
        active = []
        idle = 0
        while pending or active:
            while pending and len(active) < width:
                active.append(pending.pop(0))
            before = _progress()
            n_act = len(active)
            for g in list(active):
                try:
                    next(g)
                except StopIteration:
                    active.remove(g)
            if _progress() == before and len(active) == n_act:
                idle += 1
                if idle > 50:
                    raise RuntimeError("interleave deadlock (pool starvation): " + str([(p.name, len(p.free)) for p in _all_slots[-12:]]))
            else:
                idle = 0
            yield

    def speedup(g, k):
        while True:
            for _ in range(k):
                try:
                    r = next(g)
                except StopIteration:
                    return
                if r == "blocked":
                    break
            yield

    def run_interleaved(gens, width):
        idle = 0
        it = g_inter(gens, width)
        for _ in it:
            pass

    def _old_run_interleaved(gens, width):
        pending = list(gens)
        active = []
        while pending or active:
            while pending and len(active) < width:
                active.append(pending.pop(0))
            for g in list(active):
                try:
                    next(g)
                except StopIteration:
                    active.remove(g)

    gq_rr = [0]

    def gq(n=1):
        i = gq_rr[0]
        gq_rr[0] = (i + 1) % 4
        b = 4 + i
        return Ref(PS[b][:, 0:n * 128], ("psb", b))

    def sub(ref, ap):
        return Ref(ap, ref.keys)

    fillb = sb("fillb", [128, 512], BF16)
    R_fillb = Ref(fillb[:, :], "fillb")
    mset(R_fillb, 1.0)

    _bc = [0]

    def pe_fill(n=None):
        for _ in range(N_FILL if n is None else n):
            sc.op("pe", lambda: nc.tensor.matmul(PS[3][:, :], lhsT=ones_b[:, :], rhs=fillb[:, :], start=True, stop=True),
                  [R_onesb, R_fillb], [Ref(PS[3][:, :], ("ps", 3))], inc=False)

    class Pool_:
        def __init__(self, name, shape, dt, n):
            self.t = [sb("%s%d" % (name, i), shape, dt) for i in range(n)]
            self.name = name
            self.i = 0
            self.n = n

        def get(self):
            i = self.i
            self.i = (i + 1) % self.n
            return self.t[i], (self.name, i)

    def load_w(dst_tile, key, src_ap):
        sc.dma("pool", dst_tile, src_ap.rearrange("(c p) n -> p c n", p=128), [], [Ref(dst_tile, key)])

    def rstd_from_sumsq(ps_sum, out_rs, scale, tmp_ln):
        act(tmp_ln, ps_sum, AF.Ln, scale=scale, bias=R_eps)
        act(out_rs, tmp_ln, AF.Exp, scale=-0.5)

    eps_t = sb("eps_t", [128, 1])
    R_eps = Ref(eps_t[:, :], "eps")
    mset(R_eps, EPS)
    one_t = sb("one_t", [128, 1])
    R_one = Ref(one_t[:, :], "one")
    mset(R_one, 1.0)

    def sigmoid_to(out, in_, tmp):
        act(tmp, in_, AF.Exp, scale=-1.0)
        act(tmp, tmp, AF.Ln, bias=R_one)
        act(out, tmp, AF.Exp, scale=-1.0)

    n_persist = len(_tiles)

    def checkpoint(name, l):
        if dbg is not None and dbg == (name, l):
            raise StopBuild()

    def _layers(n_layers):
        for l in range(n_layers):
            vb = l * VL
            checkpoint("load", l)
            hT = sb("hT", [128, 8, TT], BF16)
            wstS = Slots("wst", [sb("wst%d" % i, [128, 8, 256], BF16) for i in range(3)])
            wab = sb("wab", [128, 8, 8], BF16)
            pwbd = sb("pwbd", [128, 2, 128], BF16)
            qkvT = sb("qkvT", [128, 12, TT], BF16)
            mixT = sb("mixT", [128, 8, TT], BF16)
            zs = sb("zs", [128, 4, TT], BF16)
            rawS = Slots("raw", [sb("raw%d" % i, [128, TT + 4], BF16) for i in range(3)])
            dgS = Slots("dg", [sb("dg%d" % i, [128, 128], BF16) for i in range(8)])
            accS = Slots("acc", [sb("acc%d" % i, [128, TT], F32) for i in range(2)])
            tmpS = Slots("tmpf", [sb("tmpf%d" % i, [128, TT], F32) for i in range(3)])
            sqS = Slots("sqb", [sb("sqb%d" % i, [128, TT], BF16) for i in range(2)])
            halo = sb("halo", [128, 12, 4], BF16)
            hpb = sb("hpb", [128, 2, TT + 15])
            s2b = sb("s2b", [128, TT + 15])
            s4b = sb("s4b", [128, TT + 15])
            pooledb = sb("pooledb", [128, TT], BF16)
            cbs = sb("cbs", [128, 2, TT], BF16)
            ccs = sb("ccs", [128, 2, TT], BF16)
            ub = sb("ub", [128, 2, TT + 2], BF16)
            colsA = sb("colsA", [128, 10, 16])
            colsB = sb("colsB", [128, 2, 16], BF16)
            Rl = sb("Rl", [128, 4, 2, 4])
            egl2 = sb("egl", [128, 2, 4, 2, 4])
            g4f = Slots("g4f", [sb("g4f%d" % i, [128, 4, 128], F32) for i in range(N_G4F)])
            g4b = Slots("g4b", [sb("g4b%d" % i, [128, 4, 128], BF16) for i in range(N_G4B)])
            PT = sb("PT", [128, 4, 4, 128], BF16)
            Kd = sb("Kd", [128, 4, 4, 128], BF16)
            WTb = sb("WTb", [128, 4, 4, 128], BF16)
            Ub = sb("Ub", [128, 4, 4, 128], BF16)
            qgT = sb("qgT", [128, 4, TT], BF16)
            Vn = sb("Vn", [128, 4, 128], BF16)
            S32 = sb("S32", [128, 4, 128])
            Sb = [sb("Sb%d" % i, [128, 4, 128], BF16) for i in range(2)]
            oT = sb("oT", [128, 4, TT])
            negA = sb("negA", [128, 16])
            n_mixer = len(_tiles) - n_persist

            R_halo = lambda ch: Ref(halo[:, ch, 0:3], ("halo", ch))
            for ch in range(12):
                mset(R_halo(ch), 0.0)
            R_hpb = lambda ch: Ref(hpb[:, ch, :], ("hpb", ch))
            for ch in range(2):
                sc.op("pool", lambda ch=ch: nc.gpsimd.memset(hpb[:, ch, 0:15], 0.0), [], [R_hpb(ch)])
            R_ub = lambda ch: Ref(ub[:, ch, :], ("ub", ch))
            for ch in range(2):
                sc.op("pool", lambda ch=ch: nc.gpsimd.memset(ub[:, ch, 0:2], 0.0), [], [R_ub(ch)])
            mset(Ref(S32[:, :, :], "S32"), 0.0)
            mset(Ref(Sb[0][:, :, :], ("Sb", 0)), 0.0)
            R_wab = Ref(wab[:, :, :], "wab")
            load_w(wab[:, :, :], "wab", w_in_d[l, :, 2048:2056])
            R_pw = Ref(pwbd[:, :, :], "pwbd")
            mset(R_pw, 0.0)
            for gi in range(4):
                ch, hf = gi // 2, gi % 2
                sc.dma("pool", pwbd[hf * 64:(hf + 1) * 64, ch, hf * 64:(hf + 1) * 64], pool_w_d[l, gi, :, :], [], [R_pw])
            R_negA = Ref(negA[:, :], "negA")
            act(R_negA, vcol(vb + 89, 16), AF.Exp)
            ts(R_negA, R_negA, -1.0, ALU.mult)

            def proj_chunk(wt, wkey, j, tt_):
                ps = bank()
                for c in range(8):
                    mm(ps, Ref(wt[:, c, j * 128:(j + 1) * 128], wkey), Ref(hT[:, c, :], ("hT", c)),
                       start=(c == 0), stop=(c == 7))
                return ps

            def silu_to(out, in_):
                t, k = tmpp.get()
                T = Ref(t[:, :], k)
                sigmoid_to(T, in_, T)
                tt(out, in_, T, ALU.mult)

            def make_tile(tt_):
                CA = lambda q_: Ref(colsA[:, q_, :], ("colsA", q_))
                cav = lambda q_: colsA[:, q_, :].rearrange("p (s h) -> p s h", h=4)
                egl = egl2[:, tt_ % 2, :, :, :]
                eglk = ("egl", tt_ % 2)
                R_egl = Ref(egl, eglk)

                def colsc(q_, m, h):
                    return sub(CA(q_), colsA[:, q_, m * 4 + h:m * 4 + h + 1])

                def g_norm1():
                    ti = yield from galloc(tmpS)
                    sqi = []
                    for _ in range(2):
                        _i = yield from galloc(sqS)
                        sqi.append(_i)
                    b = yield from galloc(pb)
                    ps = Ref(PS[b][:, :], ("ps", b))
                    for c in range(8):
                        SQ = Ref(sqS.tiles[sqi[c % 2]][:, :], ("sqb", sqi[c % 2]))
                        act(SQ, X(c, tt_), AF.Square)
                        mm(ps, R_onesb, SQ, start=(c == 0), stop=(c == 7), inc=True)
                        if c % 2 == 1:
                            yield
                    for i in sqi:
                        sqS.release(i)
                    RS = Ref(tmpS.tiles[ti][:, :], ("tmpf", ti))
                    act(RS, ps, AF.Ln, scale=1.0 / D, bias=R_eps)
                    pb.release(b)
                    act(RS, RS, AF.Exp, scale=-0.5)
                    yield
                    for c in range(8):
                        stt(Ref(hT[:, c, :], ("hT", c)), X(c, tt_), vcol(vb + c), RS, ALU.mult, ALU.mult)
                    tmpS.release(ti)

                F_ = lambda pool, i: Ref(pool.tiles[i][:, :], (pool.name, i))

                def proj_ps(wi, j):
                    b = yield from galloc(pb)
                    ps = Ref(PS[b][:, :], ("ps", b))
                    wt = wstS.tiles[wi]
                    for c in range(8):
                        mm(ps, Ref(wt[:, c, j * 128:(j + 1) * 128], ("wst", wi)), Ref(hT[:, c, :], ("hT", c)),
                           start=(c == 0), stop=(c == 7))
                    proj_cnt[wi] = proj_cnt.get(wi, 0) + 1
                    return b, ps

                def g_sigmoid(T, in_):
                    act(T, in_, AF.Exp, scale=-1.0)
                    yield
                    act(T, T, AF.Ln, bias=R_one)
                    yield
                    act(T, T, AF.Exp, scale=-1.0)
                    yield

                def g_qkv_chunk(wi, j, chn):
                    ri = yield from galloc(rawS)
                    b, ps = yield from proj_ps(wi, j)
                    rt = rawS.tiles[ri]
                    RAW = F_(rawS, ri)
                    cp(sub(RAW, rt[:, 0:3]), R_halo(chn), eng=RAW_ENG)
                    cp(sub(RAW, rt[:, 3:3 + TT]), ps, eng=RAW_ENG)
                    pb.release(b)
                    cp(R_halo(chn), sub(RAW, rt[:, TT:TT + 3]), eng=RAW_ENG)
                    yield
                    cw = vb + 16 + chn * 4
                    dis = []
                    for jj in range(4):
                        di = yield from galloc(dgS)
                        ts(F_(dgS, di), R_identb, vcol(cw + jj), ALU.mult, 0.0, ALU.add, eng="pool")
                        dis.append(di)
                    yield
                    ti = yield from galloc(tmpS)
                    T = F_(tmpS, ti)
                    if chn < 8:
                        ai = yield from galloc(accS)
                        ACC = F_(accS, ai)
                    bc = yield from galloc(pb)
                    psc = Ref(PS[bc][:, :], ("ps", bc))
                    for jj in range(4):
                        mm(psc, F_(dgS, dis[jj]), sub(RAW, rt[:, jj:jj + TT]), start=(jj == 0), stop=(jj == 3))
                    for di in dis:
                        dgS.release(di)
                    rawS.release(ri)
                    yield
                    yield from g_sigmoid(T, psc)
                    dst = Ref(qkvT[:, chn, :], ("qkvT", chn))
                    if chn >= 8:
                        tt(dst, psc, T, ALU.mult)
                        pb.release(bc)
                        tmpS.release(ti)
                        vdone[0] += 1
                        return
                    tt(ACC, psc, T, ALU.mult)
                    pb.release(bc)
                    yield
                    si = yield from galloc(sqS)
                    SQ = F_(sqS, si)
                    if SQ_ENG == "act":
                        act(SQ, ACC, AF.Square)
                    else:
                        tt(SQ, ACC, ACC, ALU.mult)
                    yield
                    b2 = yield from galloc(pb)
                    ps2 = Ref(PS[b2][:, :], ("ps", b2))
                    mm(ps2, R_onesb, SQ)
                    sqS.release(si)
                    yield
                    act(T, ps2, AF.Ln, scale=1.0, bias=R_eps)
                    pb.release(b2)
                    yield
                    act(T, T, AF.Exp, scale=-0.5)
                    yield
                    if chn < 4:
                        stt(dst, ACC, 128.0 ** -0.5, T, ALU.mult, ALU.mult)
                    else:
                        tt(dst, ACC, T, ALU.mult)
                    tmpS.release(ti)
                    accS.release(ai)

                def g_z_chunk(wi, j, h):
                    ti = yield from galloc(tmpS)
                    T = F_(tmpS, ti)
                    b, ps = yield from proj_ps(wi, j)
                    yield from g_sigmoid(T, ps)
                    tt(Ref(zs[:, h, :], ("zs", h)), ps, T, ALU.mult)
                    pb.release(b)
                    tmpS.release(ti)

                def g_pool_chunk(wi, ch):
                    b, ps = yield from proj_ps(wi, ch)
                    HP = R_hpb(ch)
                    cp(sub(HP, hpb[:, ch, 15:15 + TT]), ps, eng="act")
                    pb.release(b)
                    yield
                    n = TT + 15
                    R_s2 = Ref(s2b[:, :], "s2b")
                    R_s4 = Ref(s4b[:, :], "s4b")
                    tt(sub(R_s2, s2b[:, 1:n]), sub(HP, hpb[:, ch, 1:n]), sub(HP, hpb[:, ch, 0:n - 1]), ALU.add)
                    yield
                    tt(sub(R_s4, s4b[:, 3:n]), sub(R_s2, s2b[:, 3:n]), sub(R_s2, s2b[:, 1:n - 2]), ALU.add)
                    yield
                    if ch == 1:
                        tt(sub(R_s2, s2b[:, 7:n]), sub(R_s4, s4b[:, 7:n]), sub(R_s4, s4b[:, 3:n - 4]), ALU.add)
                        yield
                        tt(sub(R_s4, s4b[:, 15:n]), sub(R_s2, s2b[:, 15:n]), sub(R_s2, s2b[:, 7:n - 8]), ALU.add)
                        yield
                    ai = yield from galloc(accS)
                    at = accS.tiles[ai]
                    M = F_(accS, ai)
                    wins = (2, 4) if ch == 0 else (8, 16)
                    for hf in range(2):
                        src_t = s2b if hf == 0 else s4b
                        R_src = R_s2 if hf == 0 else R_s4
                        pp = slice(hf * 64, (hf + 1) * 64)
                        ts(sub(M, at[pp, :]), sub(R_src, src_t[pp, 15:15 + TT]), 1.0 / wins[hf], ALU.mult)
                        if tt_ == 0:
                            tt(sub(M, at[pp, 0:16]), sub(R_src, src_t[pp, 15:31]), sub(R_invc, invc[pp, ch, :]), ALU.mult)
                    yield
                    R_pooled = Ref(pooledb[:, :], "pooledb")
                    tt(R_pooled, M, sub(HP, hpb[:, ch, 15:15 + TT]), ALU.subtract)
                    accS.release(ai)
                    cp(sub(HP, hpb[:, ch, 0:15]), sub(HP, hpb[:, ch, TT:TT + 15]), eng="act")
                    yield
                    b2 = yield from galloc(pb)
                    ps2 = Ref(PS[b2][:, :], ("ps", b2))
                    mm(ps2, sub(R_pw, pwbd[:, ch, :]), R_pooled)
                    yield
                    act(Ref(mixT[:, 4 + ch, :], ("mixT", 4 + ch)), ps2, AF.Copy, scale=vcol(vb + 65 + ch))
                    pb.release(b2)

                def g_store_chunk(wi, ch, dst_t, nm):
                    b, ps = yield from proj_ps(wi, ch)
                    cp(Ref(dst_t[:, ch, :], (nm, ch)), ps, eng="act")
                    pb.release(b)
                    yield

                def g_sconv_chunk(wi, ch):
                    b, ps = yield from proj_ps(wi, ch)
                    U_ = R_ub(ch)
                    tt(sub(U_, ub[:, ch, 2:2 + TT]), Ref(ccs[:, ch, :], ("ccs", ch)), ps, ALU.mult)
                    pb.release(b)
                    yield
                    cw = vb + 67 + ch * 3
                    dis = []
                    for jj in range(3):
                        di = yield from galloc(dgS)
                        ts(F_(dgS, di), R_identb, vcol(cw + jj), ALU.mult, 0.0, ALU.add, eng="pool")
                        dis.append(di)
                    yield
                    bc = yield from galloc(pb)
                    psc = Ref(PS[bc][:, :], ("ps", bc))
                    for jj in range(3):
                        mm(psc, F_(dgS, dis[jj]), sub(U_, ub[:, ch, jj:jj + TT]), start=(jj == 0), stop=(jj == 2))
                    for di in dis:
                        dgS.release(di)
                    yield
                    cp(sub(U_, ub[:, ch, 0:2]), sub(U_, ub[:, ch, TT:TT + 2]), eng="act")
                    tt(Ref(mixT[:, 6 + ch, :], ("mixT", 6 + ch)), psc, Ref(cbs[:, ch, :], ("cbs", ch)), ALU.mult)
                    pb.release(bc)

                blocks = []
                for blk in range(6):
                    blocks.append((blk * 256, lambda wi, j, blk=blk: g_qkv_chunk(wi, j, blk * 2 + j)))
                for blk in range(2):
                    blocks.append((1536 + blk * 256, lambda wi, j, blk=blk: g_z_chunk(wi, j, blk * 2 + j)))
                blocks.append((2056, lambda wi, j: g_pool_chunk(wi, j)))
                blocks.append((2312, lambda wi, j: g_store_chunk(wi, j, cbs, "cbs")))
                blocks.append((2568, lambda wi, j: g_store_chunk(wi, j, ccs, "ccs")))
                blocks.append((2824, lambda wi, j: g_sconv_chunk(wi, j)))
                loaded = {}

                lim = [NA_BLOCKS - 1]
                vdone = [0]

                def issue_loads(upto):
                    upto = min(upto, lim[0])
                    for k in range(len(blocks)):
                        if k > upto:
                            break
                        if k not in loaded and wstS.free:
                            wi = wstS.alloc()
                            load_w(wstS.tiles[wi][:, :, :], ("wst", wi), w_in_d[l, :, blocks[k][0]:blocks[k][0] + 256])
                            loaded[k] = wi

                proj_cnt = {}

                def g_block(k):
                    issue_loads(k + 2)
                    while k not in loaded:
                        issue_loads(k)
                        if k not in loaded:
                            yield "blocked"
                    wi = loaded[k]
                    if PAIR_CHUNKS and k != 8:
                        proj_cnt[wi] = 0
                        released = False
                        for _ in g_inter([blocks[k][1](wi, 0), blocks[k][1](wi, 1)], 2):
                            if not released and proj_cnt[wi] >= 2:
                                wstS.release(wi)
                                released = True
                                issue_loads(k + 3)
                            yield
                        if not released:
                            wstS.release(wi)
                        return
                    g0 = blocks[k][1](wi, 0)
                    yield from g0
                    yield
                    g1 = blocks[k][1](wi, 1)
                    first = True
                    for _ in g1:
                        if first:
                            wstS.release(wi)
                            first = False
                            issue_loads(k + 3)
                        yield
                    if first:
                        wstS.release(wi)

                def g_cols():
                    bAB = yield from galloc(pa)
                    psAB = Ref(PS[bAB][:, 0:128], ("ps", bAB))
                    abv = psAB.ap[:, 0:32].rearrange("p (s e) -> p s e", e=8)
                    for s_ in range(4):
                        for c in range(8):
                            mm(sub(psAB, abv[:, s_, :]), Ref(hT[:, c, s_ * 128:(s_ + 1) * 128], ("hT", c)),
                               sub(R_wab, wab[:, c, :]), start=(c == 0), stop=(c == 7))
                    sc.op("dve", lambda: nc.vector.tensor_tensor(out=cav(5), in0=abv[:, :, 0:4],
                                                                 in1=vec[:, vb + 73:vb + 89].rearrange("p (s h) -> p s h", h=4),
                                                                 op=ALU.add), [psAB, VEC], [CA(5)])
                    act(CA(5), CA(5), AF.Exp)
                    act(CA(5), CA(5), AF.Ln, bias=R_one)
                    tt(CA(0), CA(5), R_negA, ALU.mult)
                    yield
                    sc.op("act", lambda: nc.scalar.activation(out=cav(6), in_=abv[:, :, 4:8], func=AF.Exp, scale=-1.0),
                          [psAB], [CA(6)])
                    act(CA(6), CA(6), AF.Ln, bias=R_one)
                    act(CA(1), CA(6), AF.Exp, scale=-1.0)
                    yield
                    pa.release(bAB)
                    bG = yield from galloc(pa)
                    psG = Ref(PS[bG][:, 0:128], ("ps", bG))
                    gcp = sub(psG, psG.ap[:, 0:16])
                    glp = sub(psG, psG.ap[:, 16:32])
                    eglp = sub(psG, psG.ap[:, 32:64])
                    mm(gcp, R_Mincl, CA(0))
                    mm(glp, R_Msame, CA(0))
                    for j in range(2):
                        sc.op("dve", lambda j=j: nc.vector.tensor_scalar(out=Rl[:, :, j, :], in0=cav(0), scalar1=ind[:, j:j + 1],
                                                                           scalar2=None, op0=ALU.mult),
                              [CA(0), R_ind], [Ref(Rl[:, :, :, :], "Rl")])
                    mm(eglp, R_onesf, Ref(Rl[:, :, :, :].rearrange("p s j h -> p (s j h)"), "Rl"))
                    ts(CA(2), gcp, -1.0, ALU.mult)
                    act(CA(7), gcp, AF.Exp)
                    tt(CA(3), CA(1), CA(7), ALU.mult)
                    cp(Ref(colsB[:, 0, :], ("colsB", 0)), CA(1), eng="dve")
                    cp(Ref(colsB[:, 1, :], ("colsB", 1)), CA(7), eng="dve")
                    yield
                    tt(CA(6), glp, CA(2), ALU.add)
                    act(CA(4), CA(6), AF.Exp)
                    act(Ref(egl2[:, tt_ % 2, :, :, :].rearrange("p s j h -> p (s j h)"), eglk), eglp, AF.Exp)
                    pa.release(bG)


                B4 = lambda t: t[:, :, :]
                MsT_b4 = Ref(MsT[:, :].unsqueeze(1).to_broadcast([128, 4, 128]), "MsT")
                idf_b4 = Ref(ident_f[:, :].unsqueeze(1).to_broadcast([128, 4, 128]), "ident_f")

                def psv(b):
                    return PS[b][:, :].rearrange("p (h t) -> p h t", h=4)

                def mm4(b, lfn, rfn):
                    pe_fill()
                    if BURST_EVERY:
                        _bc[0] += 1
                        if _bc[0] % BURST_EVERY == 0:
                            pe_fill(BURST_N)
                    P_ = Ref(psv(b), ("ps", b))
                    for h in range(4):
                        mm(sub(P_, PS[b][:, h * 128:(h + 1) * 128]), lfn(h), rfn(h))
                    return P_

                done = {}

                def gen_pre(m):
                    cols = slice(m * 128, (m + 1) * 128)
                    qT = lambda h: Ref(qkvT[:, h, cols], ("qkvT", h))
                    kT = lambda h: Ref(qkvT[:, 4 + h, cols], ("qkvT", 4 + h))
                    vT = lambda h: Ref(qkvT[:, 8 + h, cols], ("qkvT", 8 + h))
                    qT4 = Ref(qkvT[:, 0:4, cols], [("qkvT", h) for h in range(4)])
                    csb = lambda q_: Ref(colsA[:, q_, m * 4:(m + 1) * 4].unsqueeze(2).to_broadcast([128, 4, 128]),
                                         ("colsA", q_))
                    f4 = lambda i: Ref(g4f.tiles[i][:, :, :], ("g4f", i))
                    b4 = lambda i: Ref(g4b.tiles[i][:, :, :], ("g4b", i))
                    f4h = lambda i, h: Ref(g4f.tiles[i][:, h, :], ("g4f", i))
                    b4h = lambda i, h: Ref(g4b.tiles[i][:, h, :], ("g4b", i))
                    gL = lambda h: Ref(colsA[:, 0, m * 4 + h:m * 4 + h + 1].to_broadcast([128, 128]), ("colsA", 0))
                    bL = lambda h: Ref(colsB[:, 0, m * 4 + h:m * 4 + h + 1].to_broadcast([128, 128]), ("colsB", 0))
                    eL = lambda h: Ref(colsB[:, 1, m * 4 + h:m * 4 + h + 1].to_broadcast([128, 128]), ("colsB", 1))
                    b = yield from galloc(pa)
                    psG_ = Ref(psv(b), ("ps", b))
                    for h in range(4):
                        o_ = sub(psG_, PS[b][:, h * 128:(h + 1) * 128])
                        mm(o_, gL(h), R_Mincl, start=True, stop=False)
                        mm(o_, R_identb, R_NEG, start=False, stop=True)
                    iE2 = yield from galloc(g4f)
                    for h in range(4):
                        act(f4h(iE2, h), sub(psG_, PS[b][:, h * 128:(h + 1) * 128]), AF.Exp, bias=colsc(2, m, h))
                    pa.release(b)
                    yield
                    b = yield from galloc(pa)
                    psE_ = mm4(b, eL, lambda h: R_identb)
                    tt(Ref(qgT[:, :, cols], ("qgT", m)), qT4, psE_, ALU.mult)
                    pa.release(b)
                    yield
                    iGT = yield from galloc(g4f)
                    tt(f4(iGT), f4(iE2), MsT_b4, ALU.mult)
                    b = yield from galloc(pa)
                    psB_ = mm4(b, bL, lambda h: R_identb)
                    tt(f4(iGT), f4(iGT), psB_, ALU.mult)
                    pa.release(b)
                    yield
                    b = yield from galloc(pa)
                    psK_ = mm4(b, kT, kT)
                    iB = yield from galloc(g4b)
                    tt(b4(iB), psK_, f4(iGT), ALU.mult)
                    pa.release(b)
                    g4f.release(iGT)
                    iR = yield from galloc(g4b)
                    tt(b4(iR), idf_b4, b4(iB), ALU.subtract)
                    yield
                    b = yield from galloc(pa)
                    psQ_ = mm4(b, kT, qT)
                    tt(Ref(PT[:, :, m, :], ("PT", m)), psQ_, f4(iE2), ALU.mult)
                    pa.release(b)
                    g4f.release(iE2)
                    yield
                    b = yield from galloc(pa)
                    ps_ = mm4(b, lambda h: b4h(iB, h), lambda h: R_identb)
                    iA = yield from galloc(g4b)
                    cp(b4(iA), ps_, eng="act")
                    pa.release(b)
                    yield
                    b = yield from galloc(pa)
                    ps_ = mm4(b, lambda h: b4h(iA, h), lambda h: b4h(iB, h))
                    iBn = yield from galloc(g4b)
                    cp(b4(iBn), ps_, eng=B1_ENG)
                    pa.release(b)
                    yield
                    b = yield from galloc(pa)
                    ps_ = mm4(b, lambda h: b4h(iB, h), lambda h: b4h(iA, h))
                    iAn = yield from galloc(g4b)
                    cp(b4(iAn), ps_, eng="act")
                    pa.release(b)
                    g4b.release(iB)
                    g4b.release(iA)
                    iB, iA = iBn, iAn
                    yield
                    for j in range(1, 6):
                        b = yield from galloc(pa)
                        ps_ = mm4(b, lambda h: b4h(iA, h), lambda h: b4h(iR, h))
                        iRn = yield from galloc(g4b)
                        tt(b4(iRn), b4(iR), ps_, ALU.add)
                        pa.release(b)
                        g4b.release(iR)
                        iR = iRn
                        yield
                        if j < 5:
                            b = yield from galloc(pa)
                            ps_ = mm4(b, lambda h: b4h(iA, h), lambda h: b4h(iB, h))
                            iBn = yield from galloc(g4b)
                            cp(b4(iBn), ps_, eng="act")
                            pa.release(b)
                            yield
                            b = yield from galloc(pa)
                            ps_ = mm4(b, lambda h: b4h(iB, h), lambda h: b4h(iA, h))
                            iAn = yield from galloc(g4b)
                            cp(b4(iAn), ps_, eng="act")
                            pa.release(b)
                            g4b.release(iB)
                            g4b.release(iA)
                            iB, iA = iBn, iAn
                            yield
                    g4b.release(iB)
                    g4b.release(iA)
                    b = yield from galloc(pa)
                    ps_ = mm4(b, kT, lambda h: R_identb)
                    iKT = yield from galloc(g4b)
                    tt(b4(iKT), ps_, csb(3), ALU.mult)
                    tt(Ref(Kd[:, :, m, :], ("Kd", m)), ps_, csb(4), ALU.mult)
                    pa.release(b)
                    yield
                    while vdone[0] < 4:
                        yield "blocked"
                    b = yield from galloc(pa)
                    ps_ = mm4(b, vT, lambda h: R_identb)
                    iVT = yield from galloc(g4b)
                    tt(b4(iVT), ps_, csb(1), ALU.mult)
                    pa.release(b)
                    yield
                    b = yield from galloc(pa)
                    ps_ = mm4(b, lambda h: b4h(iKT, h), lambda h: b4h(iR, h))
                    cp(Ref(WTb[:, :, m, :], ("WTb", m)), ps_, eng="act")
                    pa.release(b)
                    g4b.release(iKT)
                    yield
                    b = yield from galloc(pa)
                    ps_ = mm4(b, lambda h: b4h(iR, h), lambda h: b4h(iVT, h))
                    cp(Ref(Ub[:, :, m, :], ("Ub", m)), ps_, eng=B1_ENG)
                    pa.release(b)
                    g4b.release(iVT)
                    g4b.release(iR)
                    done[m] = True

                def g_scan():
                    R_S32a = Ref(S32[:, :, :], "S32")
                    R_Sba = lambda i: Ref(Sb[i][:, :, :], ("Sb", i))
                    for n in range(8):
                        m, j = n // 2, n % 2
                        while m not in done:
                            yield "blocked"
                        pp = slice(j * 64, (j + 1) * 64)
                        gch = tt_ * 8 + n
                        cur, nxt = gch % 2, (gch + 1) % 2
                        egb = Ref(egl2[:, tt_ % 2, m, j, :].unsqueeze(2).to_broadcast([128, 4, 128]), eglk)
                        R_Vn = Ref(Vn[pp, :, :], "Vn")
                        b1 = yield from galloc(pa)
                        psS = mm4(b1, lambda h: Ref(WTb[:, h, m, :], ("WTb", m)), lambda h: Ref(Sb[cur][:, h, :], ("Sb", cur)))
                        tt(R_Vn, Ref(Ub[pp, :, m, :], ("Ub", m)), sub(psS, psv(b1)[pp, :, :]), ALU.subtract)
                        pa.release(b1)
                        tt(R_S32a, R_S32a, egb, ALU.mult)
                        yield
                        b2 = yield from galloc(pa)
                        psO = Ref(PS[b2][:, 0:256].rearrange("p (h t) -> p h t", h=4), ("ps", b2))
                        for h in range(4):
                            o_ = sub(psO, PS[b2][:, h * 64:(h + 1) * 64])
                            mm(o_, Ref(Sb[cur][:, h, :], ("Sb", cur)), Ref(qgT[:, h, n * 64:(n + 1) * 64], ("qgT", m)),
                               start=True, stop=False)
                            mm(o_, Ref(Vn[pp, h, :], "Vn"), Ref(PT[pp, h, m, j * 64:(j + 1) * 64], ("PT", m)),
                               start=False, stop=True)
                        cp(Ref(oT[:, :, n * 64:(n + 1) * 64], ("oT", n)), psO, eng="act")
                        pa.release(b2)
                        yield
                        b3 = yield from galloc(pa)
                        psU = mm4(b3, lambda h: Ref(Kd[pp, h, m, :], ("Kd", m)), lambda h: Ref(Vn[pp, h, :], "Vn"))
                        tt(R_Sba(nxt), R_S32a, psU, ALU.add)
                        tt(R_S32a, R_S32a, psU, ALU.add)
                        pa.release(b3)
                        yield

                def g_onorm(h):
                    R_o = Ref(oT[:, h, :], [("oT", n) for n in range(8)])
                    ti = yield from galloc(tmpS)
                    si = yield from galloc(sqS)
                    SQ = Ref(sqS.tiles[si][:, :], ("sqb", si))
                    act(SQ, R_o, AF.Square)
                    yield
                    b = yield from galloc(pb)
                    ps2 = Ref(PS[b][:, :], ("ps", b))
                    mm(ps2, R_onesb, SQ)
                    sqS.release(si)
                    yield
                    RS = Ref(tmpS.tiles[ti][:, :], ("tmpf", ti))
                    act(RS, ps2, AF.Ln, scale=1.0 / 128, bias=R_eps)
                    pb.release(b)
                    yield
                    act(RS, RS, AF.Exp, scale=-0.5)
                    yield
                    stt(RS, R_o, vcol(vb + 64), RS, ALU.mult, ALU.mult)
                    yield
                    tt(Ref(mixT[:, h, :], ("mixT", h)), RS, Ref(zs[:, h, :], ("zs", h)), ALU.mult)
                    tmpS.release(ti)

                def dbg_dumps():
                    pass
                    if dbg is not None and dbg == ("qkv", l):
                        for c in range(8):
                            sc.dma("pool", dbg_d[c * 128:(c + 1) * 128, tt_ * TT:(tt_ + 1) * TT], qkvT[:, c, :],
                                   [Ref(qkvT[:, c, :], ("qkvT", c))], [])
                    if dbg is not None and dbg == ("o", l):
                        for c in range(4):
                            sc.dma("sp", dbg_d[c * 128:(c + 1) * 128, tt_ * TT:(tt_ + 1) * TT], oT[:, c, :],
                                   [Ref(oT[:, c, :], [("oT", n) for n in range(8)])], [])
                    if dbg is not None and dbg == ("mixed", l):
                        for c in range(8):
                            sc.dma("pool", dbg_d[c * 128:(c + 1) * 128, tt_ * TT:(tt_ + 1) * TT], mixT[:, c, :],
                                   [Ref(mixT[:, c, :], ("mixT", c))], [])
                def g_wout():
                    for blk in range(4):
                        wi = yield from galloc(wstS)
                        wt = wstS.tiles[wi]
                        load_w(wt[:, :, :], ("wst", wi), w_out_d[l, :, blk * 256:(blk + 1) * 256])
                        bs = []
                        for j in range(2):
                            b = yield from galloc(pb)
                            bs.append(b)
                        for j in range(2):
                            ps = Ref(PS[bs[j]][:, :], ("ps", bs[j]))
                            for c in range(8):
                                mm(ps, Ref(wt[:, c, j * 128:(j + 1) * 128], ("wst", wi)), Ref(mixT[:, c, :], ("mixT", c)),
                                   start=(c == 0), stop=(c == 7))
                        wstS.release(wi)
                        yield
                        for j in range(2):
                            o_ = blk * 2 + j
                            ps = Ref(PS[bs[j]][:, :], ("ps", bs[j]))
                            tt(X(o_, tt_), X(o_, tt_), ps, ALU.add)
                            pb.release(bs[j])
                        yield

                def gA():
                    yield from g_norm1()
                    if COLS_IN_A:
                        yield from g_inter([g_cols()] + [g_block(k) for k in range(NA_BLOCKS)], CHUNK_WIDTH + 1)
                    else:
                        yield from g_inter([g_block(k) for k in range(NA_BLOCKS)], CHUNK_WIDTH)

                def g_gdn_pre():
                    if not COLS_IN_A:
                        yield from g_cols()
                    yield from g_inter([gen_pre(m) for m in range(4)], PRE_WIDTH)

                def gB():
                    lim[0] = len(blocks) - 1
                    gl = [g_gdn_pre()]
                    if SCAN_IN_B:
                        gl.append(g_scan())
                    yield from g_inter(gl + [g_block(k) for k in range(NA_BLOCKS, len(blocks))], len(gl) + 2)

                def gC():
                    if not SCAN_IN_B:
                        yield from g_scan()
                    yield from g_inter([g_onorm(h) for h in range(4)], 2)
                    dbg_dumps()
                    yield from g_wout()

                return gA, gB, gC

            tiles_ = [make_tile(t) for t in range(NT)]
            run_interleaved([tiles_[0][0]()], 1)
            run_interleaved([tiles_[0][1]()], 1)
            for t in range(NT):
                if t + 1 < NT:
                    run_interleaved([speedup(tiles_[t][2](), C_SPEED), speedup(tiles_[t + 1][0](), A_SPEED)], 2)
                    run_interleaved([tiles_[t + 1][1]()], 1)
                else:
                    run_interleaved([tiles_[t][2]()], 1)

            sc.barrier()
            free_tiles(n_mixer)
            checkpoint("x1", l)
            checkpoint("mixed", l)
            checkpoint("qkv", l)
            checkpoint("o", l)

            h2T = sb("h2T", [128, 8, S], BF16)
            ffp = Pool_("ffT", [128, 2, S], BF16, 2)
            wgp = Pool_("wg", [128, 8, 256], BF16, 2)
            wup = Pool_("wu", [128, 8, 256], BF16, 2)
            wdp = Pool_("wd", [128, 2, D], BF16, 2)
            tmpp = Pool_("tmpf", [128, TT], F32, 4)
            sqp = Pool_("sqb", [128, TT], BF16, 2)
            pTb = sb("pTb", [128, 2, S], BF16)
            ppw = sb("ppw", [128, 2, D], BF16)
            pgp = Pool_("pg", [128, 8, 256], BF16, 2)
            n_ffn = len(_tiles) - n_persist
            H2 = lambda c, t: Ref(h2T[:, c, t * TT:(t + 1) * TT], ("h2T", c, t))

            for c2 in range(2):
                for hf in range(2):
                    sc.dma("pool", pTb[:, c2, hf * 1024:(hf + 1) * 1024],
                           pT_d[l, c2 * 128:(c2 + 1) * 128, hf * 1024:(hf + 1) * 1024], [], [Ref(pTb[:, c2, :], ("pTb", c2, hf))])
            load_w(ppw[:, :, :], "ppw", ple_proj_d[l, :, :])

            for tt_ in range(NT):
                ps = bank()
                for c in range(8):
                    t, k = sqp.get()
                    SQ = Ref(t[:, :], k)
                    act(SQ, X(c, tt_), AF.Square)
                    mm(ps, R_onesb, SQ, start=(c == 0), stop=(c == 7), inc=True)
                t, k = tmpp.get()
                LN = Ref(t[:, :], k)
                t, k = tmpp.get()
                RS = Ref(t[:, :], k)
                rstd_from_sumsq(ps, RS, 1.0 / D, LN)
                for c in range(8):
                    stt(H2(c, tt_), X(c, tt_), vcol(vb + 8 + c), RS, ALU.mult, ALU.mult)

            for grp in range(NF // 2):
                f0 = grp * 256
                wg, wgk = wgp.get()
                load_w(wg[:, :, :], wgk, w_gate_d[l, :, f0:f0 + 256])
                wu, wuk = wup.get()
                load_w(wu[:, :, :], wuk, w_up_d[l, :, f0:f0 + 256])
                wd, wdk = wdp.get()
                load_w(wd[:, :, :], wdk, w_down_d[l, f0:f0 + 256, :])
                ff, ffk = ffp.get()
                FF = lambda fi, t: Ref(ff[:, fi, t * TT:(t + 1) * TT], (ffk, fi, t))
                for fi in range(2):
                    for tt_ in range(NT):
                        psg = bank()
                        for c in range(8):
                            mm(psg, Ref(wg[:, c, fi * 128:(fi + 1) * 128], wgk), H2(c, tt_), start=(c == 0), stop=(c == 7))
                        psu = bank()
                        for c in range(8):
                            mm(psu, Ref(wu[:, c, fi * 128:(fi + 1) * 128], wuk), H2(c, tt_), start=(c == 0), stop=(c == 7))
                        t, k = tmpp.get()
                        SG = Ref(t[:, :], k)
                        sigmoid_to(SG, psg, SG)
                        tt(SG, SG, psg, ALU.mult)
                        tt(FF(fi, tt_), SG, psu, ALU.mult)
                for o_ in range(8):
                    for tt_ in range(NT):
                        ps = bank()
                        for fi in range(2):
                            mm(ps, Ref(wd[:, fi, o_ * 128:(o_ + 1) * 128], wdk), FF(fi, tt_), start=(fi == 0), stop=(fi == 1))
                        tt(X(o_, tt_), X(o_, tt_), ps, ALU.add)

            checkpoint("x2", l)
            for c in range(8):
                for tt_ in range(NT):
                    cp(H2(c, tt_), X(c, tt_), eng=("act" if (c + tt_) % 2 else "dve"))
            for blk in range(4):
                pg, pgk = pgp.get()
                load_w(pg[:, :, :], pgk, ple_gate_d[l, :, blk * 256:(blk + 1) * 256])
                for j in range(2):
                    o_ = blk * 2 + j
                    for tt_ in range(NT):
                        psg = bank()
                        for c in range(8):
                            mm(psg, Ref(pg[:, c, j * 128:(j + 1) * 128], pgk), H2(c, tt_), start=(c == 0), stop=(c == 7))
                        psp = bank()
                        for c2 in range(2):
                            mm(psp, Ref(ppw[:, c2, o_ * 128:(o_ + 1) * 128], "ppw"),
                               Ref(pTb[:, c2, tt_ * TT:(tt_ + 1) * TT], ("pTb", c2, tt_ // 2)), start=(c2 == 0), stop=(c2 == 1))
                        t, k = tmpp.get()
                        SG = Ref(t[:, :], k)
                        sigmoid_to(SG, psg, SG)
                        tt(SG, SG, psp, ALU.mult)
                        tt(X(o_, tt_), X(o_, tt_), SG, ALU.add)
            sc.barrier()
            free_tiles(n_ffn)


    def _final():
        if 'final' in _skip:
            for c in range(8):
                for t in range(NT):
                    sc.dma("sp", yT_d[c * 128:(c + 1) * 128, t * TT:(t + 1) * TT], xT[:, c, t * TT:(t + 1) * TT], [X(c, t)], [])
            return
        tmpp = Pool_("tmpf", [128, TT], F32, 4)
        sqp = Pool_("sqb", [128, TT], BF16, 2)
        outp = Pool_("outb", [128, TT], F32, 4)
        if dbg is not None and dbg[0] not in ("mixed", "qkv", "o"):
            for c in range(8):
                for t in range(NT):
                    sc.dma("sp", dbg_d[c * 128:(c + 1) * 128, t * TT:(t + 1) * TT], xT[:, c, t * TT:(t + 1) * TT], [X(c, t)], [])
        for tt_ in range(NT):
            ps = bank()
            for c in range(8):
                t, k = sqp.get()
                SQ = Ref(t[:, :], k)
                act(SQ, X(c, tt_), AF.Square)
                mm(ps, R_onesb, SQ, start=(c == 0), stop=(c == 7), inc=True)
            t, k = tmpp.get()
            LN = Ref(t[:, :], k)
            t, k = tmpp.get()
            RS = Ref(t[:, :], k)
            rstd_from_sumsq(ps, RS, 1.0 / D, LN)
            for c in range(8):
                t, k = outp.get()
                OB = Ref(t[:, :], k)
                stt(OB, X(c, tt_), vcol(2 * VL + c), RS, ALU.mult, ALU.mult)
                sc.dma("sp", yT_d[c * 128:(c + 1) * 128, tt_ * TT:(tt_ + 1) * TT], t[:, :], [OB], [])
    try:
        _layers(n_layers)
    except StopBuild:
        sc.barrier()
        free_tiles(len(_tiles) - n_persist)
    _final()
    sc.finish()
    return nc


def pack_vec(inp):
    vec = np.zeros((128, NV), np.float32)
    for l in range(2):
        b = l * VL
        vec[:, b:b + 8] = inp["norm1_g"][l].reshape(8, 128).T
        vec[:, b + 8:b + 16] = inp["norm2_g"][l].reshape(8, 128).T
        cq = inp["conv_qkv"][l]
        vec[:, b + 16:b + 64] = cq.reshape(4, 12, 128).transpose(2, 1, 0).reshape(128, 48)
        vec[:, b + 64] = inp["onorm_g"][l]
        vec[:, b + 65:b + 67] = inp["pool_scale"][l].reshape(2, 128).T
        sw = inp["sconv_w"][l]
        vec[:, b + 67:b + 73] = sw.reshape(3, 2, 128).transpose(2, 1, 0).reshape(128, 6)
        vec[:, b + 73:b + 89] = np.broadcast_to(np.tile(inp["dt_bias"][l], 4)[None, :], (128, 16))
        vec[:, b + 89:b + 105] = np.broadcast_to(np.tile(inp["a_log"][l], 4)[None, :], (128, 16))
    vec[:, 2 * VL:2 * VL + 8] = inp["final_g"].reshape(8, 128).T
    return vec


_NC_CACHE = {}


def kernel(**inputs):
    inp = {k: np.asarray(v) for k, v in inputs.items()}
    x = inp["x"].astype(np.float32, copy=False)
    p = inp["p"].astype(np.float32, copy=False)
    vec = pack_vec(inp)
    shared = {
        "w_in": np.ascontiguousarray(inp["w_in"], np.float32),
        "w_out": np.ascontiguousarray(inp["w_out"], np.float32),
        "w_gate": np.ascontiguousarray(inp["w_gate"], np.float32),
        "w_up": np.ascontiguousarray(inp["w_up"], np.float32),
        "w_down": np.ascontiguousarray(inp["w_down"], np.float32),
        "ple_proj": np.ascontiguousarray(inp["ple_proj"], np.float32),
        "ple_gate": np.ascontiguousarray(inp["ple_gate"], np.float32),
        "pool_w": np.ascontiguousarray(inp["pool_w"], np.float32),
        "vec": vec,
    }
    in_maps = []
    for b in range(N_CORES):
        m = dict(shared)
        m["xT"] = np.ascontiguousarray(x[b].T)
        m["pT"] = np.ascontiguousarray(p[:, b].transpose(0, 2, 1))
        in_maps.append(m)
    if "nc" not in _NC_CACHE:
        _NC_CACHE["nc"] = build()
    res = run_bass_kernel_spmd(_NC_CACHE["nc"], in_maps, core_ids=list(range(N_CORES)))
    out = np.stack([np.ascontiguousarray(r["yT"].T) for r in res.results], axis=0)
    return out.astype(np.float32, copy=False)
```
